# Optimizing a Trainium2 kernel written in Bass

```python
import math
import jax, jax.numpy as jnp
from jax import lax
import numpy as np

D_MODEL = 1024
BATCH = 16
SEQ = 2048
DEPTH = 2

N_A_LAYERS = DEPTH // 2
N_B_LAYERS = DEPTH - N_A_LAYERS

RET_HEADS = 4
RET_QK_DIM = D_MODEL // RET_HEADS
RET_V_DIM = 2 * RET_QK_DIM
RET_CHUNK = 128
ROPE_BASE = 10000.0

SB_HEADS = 16
SB_HEAD_DIM = D_MODEL // SB_HEADS
SB_BLOCK = 128

D_FF = ((8 * D_MODEL // 3 + 127) // 128) * 128
CONV_WIDTH = 3
EPS = 1e-6

kernel_name = 'yoco_retention_stickbreaking_convffn_adaln'


def rms_norm(x, gain):
    x32 = x.astype(jnp.float32)
    y = x32 * lax.rsqrt(jnp.mean(x32 * x32, axis=-1, keepdims=True) + EPS)
    return (y * gain.astype(jnp.float32)).astype(x.dtype)


def modulate(xn, shift, scale):
    return (xn * (1.0 + scale[:, None, :]) + shift[:, None, :]).astype(xn.dtype)


def rotary(x, positions):
    half = x.shape[-1] // 2
    inv = ROPE_BASE ** (-jnp.arange(half, dtype=jnp.float32) / half)
    ang = positions.astype(jnp.float32)[..., None] * inv
    cos = jnp.cos(ang)[:, :, None, :]
    sin = jnp.sin(ang)[:, :, None, :]
    x32 = x.astype(jnp.float32)
    x1, x2 = x32[..., :half], x32[..., half:]
    return jnp.concatenate([x1 * cos - x2 * sin, x1 * sin + x2 * cos], axis=-1)


def retention(h, positions, w_in, w_out):
    B, S, _ = h.shape
    H, dk, dv, C = RET_HEADS, RET_QK_DIM, RET_V_DIM, RET_CHUNK
    proj = h @ w_in
    q, k, v, g = jnp.split(proj, [H * dk, 2 * H * dk, 2 * H * dk + H * dv], axis=-1)
    q = rotary(q.reshape(B, S, H, dk), positions)
    k = rotary(k.reshape(B, S, H, dk), positions) * (dk ** -0.5)
    v = v.reshape(B, S, H, dv).astype(jnp.float32)
    log_gamma = jnp.log(1.0 - 2.0 ** (-5.0 - jnp.arange(H, dtype=jnp.float32)))
    N = S // C

    def to_chunks(t):
        d = t.shape[-1]
        return t.reshape(B, N, C, H, d).transpose(1, 0, 3, 2, 4)

    idx = jnp.arange(C, dtype=jnp.float32)
    rel = idx[:, None] - idx[None, :]
    intra = jnp.where(rel[None] >= 0, jnp.exp(jnp.maximum(rel, 0.0)[None] * log_gamma[:, None, None]), 0.0)
    q_decay = jnp.exp((idx + 1.0)[None, :] * log_gamma[:, None])
    k_decay = jnp.exp((C - 1.0 - idx)[None, :] * log_gamma[:, None])
    chunk_decay = jnp.exp(C * log_gamma)

    def step(state, xs):
        qc, kc, vc = xs
        scores = jnp.einsum('bhnd,bhmd->bhnm', qc, kc) * intra[None]
        inner = jnp.einsum('bhnm,bhmv->bhnv', scores, vc)
        cross = jnp.einsum('bhnd,bhdv->bhnv', qc * q_decay[None, :, :, None], state)
        state = state * chunk_decay[None, :, None, None] + jnp.einsum(
            'bhmd,bhmv->bhdv', kc * k_decay[None, :, :, None], vc)
        return state, inner + cross

    state0 = jnp.zeros((B, H, dk, dv), jnp.float32)
    _, o = lax.scan(step, state0, (to_chunks(q), to_chunks(k), to_chunks(v)))
    o = o.transpose(1, 0, 3, 2, 4).reshape(B, S, H, dv)
    o = o * lax.rsqrt(jnp.mean(o * o, axis=-1, keepdims=True) + EPS)
    o = o.reshape(B, S, H * dv).astype(h.dtype)
    return (jax.nn.silu(g) * o) @ w_out


def shared_kv(h, c_act, kv_ada_w, kv_ada_b, kv_norm_g, w_kv, k_norm_g):
    B, S, _ = h.shape
    shift, scale = jnp.split(c_act @ kv_ada_w + kv_ada_b, 2, axis=-1)
    hn = modulate(rms_norm(h, kv_norm_g), shift, scale)
    k, v = jnp.split(hn @ w_kv, 2, axis=-1)
    k = rms_norm(k.reshape(B, S, SB_HEADS, SB_HEAD_DIM), k_norm_g)
    v = v.reshape(B, S, SB_HEADS, SB_HEAD_DIM)
    return k.transpose(0, 2, 1, 3), v.transpose(0, 2, 1, 3)


def stick_breaking(h, k, v, w_q, q_gain, w_out):
    B, S, _ = h.shape
    H, dh, BLK = SB_HEADS, SB_HEAD_DIM, SB_BLOCK
    NB = S // BLK
    q = rms_norm((h @ w_q).reshape(B, S, H, dh), q_gain).transpose(0, 2, 1, 3)
    qb = q.reshape(B, H, NB, BLK, dh).transpose(2, 0, 1, 3, 4)
    k32 = k.astype(jnp.float32)
    v32 = v.astype(jnp.float32)
    kpos = jnp.arange(S)
    scale = dh ** -0.5

    def block(args):
        qblk, i = args
        qpos = i * BLK + jnp.arange(BLK)
        mask = kpos[None, :] < qpos[:, None]
        z = jnp.einsum('bhqd,bhkd->bhqk', qblk.astype(jnp.float32), k32) * scale
        log_beta = jax.nn.log_sigmoid(z)
        log_1mb = jnp.where(mask, log_beta - z, 0.0)
        between = lax.cumsum(log_1mb, axis=3, reverse=True) - log_1mb
        A = jnp.where(mask, jnp.exp(log_beta + between), 0.0)
        return jnp.einsum('bhqk,bhkd->bhqd', A, v32)

    o = lax.map(block, (qb, jnp.arange(NB)))
    o = o.transpose(1, 0, 3, 2, 4).reshape(B, S, H * dh).astype(h.dtype)
    return o @ w_out


def conv_ffn(h, w_in, conv_w, conv_b, w_out):
    S = h.shape[1]
    u = h @ w_in
    up = jnp.pad(u, ((0, 0), (CONV_WIDTH - 1, 0), (0, 0)))
    y = conv_b[None, None, :] + up[:, 0:S] * conv_w[0]
    for tap in range(1, CONV_WIDTH):
        y = y + up[:, tap:tap + S] * conv_w[tap]
    val, gate = jnp.split(y, 2, axis=-1)
    return (val * jax.nn.silu(gate)) @ w_out


def setup_inputs(seed: int = 0) -> dict:
    key = jax.random.key(seed)
    ks = jax.random.split(key, 24)
    D, F = D_MODEL, D_FF
    f32 = jnp.float32

    def nrm(k, shape, fan_in, mult=1.0):
        return jax.random.normal(k, shape, f32) * (mult * fan_in ** -0.5)

    def gain(k, shape):
        return 1.0 + 0.02 * jax.random.normal(k, shape, f32)

    ret_in_cols = 2 * RET_HEADS * RET_QK_DIM + 2 * RET_HEADS * RET_V_DIM
    offsets = jax.random.randint(ks[2], (BATCH, 1), 0, 1024, dtype=jnp.int32)
    positions = (offsets + jnp.arange(SEQ, dtype=jnp.int32)[None, :]).astype(jnp.int32)
    return {
        'x': jax.random.normal(ks[0], (BATCH, SEQ, D), f32),
        'c': jax.random.normal(ks[1], (BATCH, D), f32),
        'positions': positions,
        'ada_w': nrm(ks[3], (DEPTH, D, 6 * D), D, 0.5),
        'ada_b': 0.02 * jax.random.normal(ks[4], (DEPTH, 6 * D), f32),
        'norm_mix_g': gain(ks[5], (DEPTH, D)),
        'norm_ffn_g': gain(ks[6], (DEPTH, D)),
        'ret_w_in': nrm(ks[7], (N_A_LAYERS, D, ret_in_cols), D),
        'ret_w_out': nrm(ks[8], (N_A_LAYERS, RET_HEADS * RET_V_DIM, D), RET_HEADS * RET_V_DIM),
        'kv_ada_w': nrm(ks[9], (D, 2 * D), D, 0.5),
        'kv_ada_b': 0.02 * jax.random.normal(ks[10], (2 * D,), f32),
        'kv_norm_g': gain(ks[11], (D,)),
        'w_kv': nrm(ks[12], (D, 2 * D), D),
        'k_norm_g': gain(ks[13], (SB_HEAD_DIM,)),
        'sb_w_q': nrm(ks[14], (N_B_LAYERS, D, D), D),
        'q_norm_g': gain(ks[15], (N_B_LAYERS, SB_HEAD_DIM)),
        'sb_w_out': nrm(ks[16], (N_B_LAYERS, D, D), D),
        'ffn_w_in': nrm(ks[17], (DEPTH, D, 2 * F), D),
        'ffn_conv_w': nrm(ks[18], (DEPTH, CONV_WIDTH, 2 * F), CONV_WIDTH),
        'ffn_conv_b': 0.02 * jax.random.normal(ks[19], (DEPTH, 2 * F), f32),
        'ffn_w_out': nrm(ks[20], (DEPTH, F, D), F),
    }


def reference(x, c, positions, ada_w, ada_b, norm_mix_g, norm_ffn_g, ret_w_in, ret_w_out,
              kv_ada_w, kv_ada_b, kv_norm_g, w_kv, k_norm_g, sb_w_q, q_norm_g, sb_w_out,
              ffn_w_in, ffn_conv_w, ffn_conv_b, ffn_w_out):
    c_act = jax.nn.silu(c)
    mods = jnp.einsum('bd,lde->lbe', c_act, ada_w) + ada_b[:, None, :]
    h = x
    k_sh = None
    v_sh = None
    for layer in range(DEPTH):
        shift_m, scale_m, gate_m, shift_f, scale_f, gate_f = jnp.split(mods[layer], 6, axis=-1)
        hn = modulate(rms_norm(h, norm_mix_g[layer]), shift_m, scale_m)
        if layer < N_A_LAYERS:
            mix = retention(hn, positions, ret_w_in[layer], ret_w_out[layer])
        else:
            j = layer - N_A_LAYERS
            if j == 0:
                k_sh, v_sh = shared_kv(h, c_act, kv_ada_w, kv_ada_b, kv_norm_g, w_kv, k_norm_g)
            mix = stick_breaking(hn, k_sh, v_sh, sb_w_q[j], q_norm_g[j], sb_w_out[j])
        h = h + (gate_m[:, None, :] * mix).astype(h.dtype)
        hf = modulate(rms_norm(h, norm_ffn_g[layer]), shift_f, scale_f)
        ff = conv_ffn(hf, ffn_w_in[layer], ffn_conv_w[layer], ffn_conv_b[layer], ffn_w_out[layer])
        h = h + (gate_f[:, None, :] * ff).astype(h.dtype)
    return h
```

```python
import contextlib
import math
import numpy as np
import concourse.bass as bass
import concourse.mybir as mybir
from concourse.bass_utils import run_bass_kernel_spmd

F32 = mybir.dt.float32
BF16 = mybir.dt.bfloat16
I32 = mybir.dt.int32
AF = mybir.ActivationFunctionType
ALU = mybir.AluOpType

NCORES = 8
D = 1024
T = 512
RET_H = 4
SB_H = 16
FF = 2816
NFC = 22
EPS = 1e-6
GAMMAS = [1.0 - 2.0 ** (-5.0 - h) for h in range(RET_H)]
NSLOT = 4
NF = 11
NBF = 32
NCVT = 8
MAGIC = 12582912.0
TWO_PI = 2.0 * math.pi
NFILL = 0
SAME_ENG_GAP = None

CB_ID, CB_ONES, CB_BD, CB_SEL, CB_NEGU, CB_NEGONES, CB_ZERO, CB_DMASK = [i * 128 for i in range(8)]

VC_ADAB = 0
VC_NMG = 112
VC_NFG = 128
VC_KVG = 144
VC_CONVB = 152
VC_KG = 240
VC_QG = 241
VC_CONVW = 256


class Buf:
    __slots__ = ("w", "r")

    def __init__(self):
        self.w = None
        self.r = []


class Sched:
    ENG = ("pe", "act", "dve", "pool", "sp")

    def __init__(self, nc, es):
        self.nc = nc
        self.es = es
        self.eng = {"pe": nc.tensor, "act": nc.scalar, "dve": nc.vector,
                    "pool": nc.gpsimd, "sp": nc.sync}
        self.sems = {}
        self.cnt = {}
        for e in self.ENG:
            self.sems[e] = es.enter_context(nc.semaphore("sem_" + e))
            self.cnt[e] = 0
        self.seen = {e: {} for e in self.ENG}
        self.pending = {e: False for e in self.ENG}
        self.n_inst = {e: 0 for e in self.ENG}
        self.n_wait = {e: 0 for e in self.ENG}

    def new_sem(self, key):
        self.sems[key] = self.es.enter_context(self.nc.semaphore("sem_" + key))
        self.cnt[key] = 0
        return key

    def _need(self, e, tok, waits):
        if tok is None:
            return
        k, v = tok
        if k == e and e == "pe":
            return
        if k == e and SAME_ENG_GAP is not None and (self.cnt[e] - v) >= SAME_ENG_GAP:
            return
        if self.seen[e].get(k, 0) >= v:
            return
        if waits.get(k, 0) < v:
            waits[k] = v

    def _emit_waits(self, e, waits):
        for k, v in waits.items():
            self.eng[e].wait_ge(self.sems[k], v)
            self.seen[e][k] = v
            self.n_wait[e] += 1

    def _collect(self, e, reads, writes):
        waits = {}
        for b in reads:
            self._need(e, b.w, waits)
        for b in writes:
            self._need(e, b.w, waits)
            for t in b.r:
                self._need(e, t, waits)
        return waits

    def _record(self, tok, reads, writes):
        for b in writes:
            b.w = tok
            b.r = []
        for b in reads:
            if b in writes:
                continue
            b.r.append(tok)
            if len(b.r) > 16:
                best = {}
                for k, v in b.r:
                    if best.get(k, 0) < v:
                        best[k] = v
                b.r = list(best.items())

    def op(self, e, fn, reads=(), writes=(), signal=True):
        waits = self._collect(e, reads, writes)
        if e in waits and waits[e] > self.cnt[e]:
            raise RuntimeError("self-wait on unsignalled op")
        self._emit_waits(e, waits)
        ins = fn(self.eng[e])
        self.n_inst[e] += 1
        if signal:
            self.cnt[e] += 1
            ins.then_inc(self.sems[e], 1)
            self.pending[e] = False
            tok = (e, self.cnt[e])
        else:
            self.pending[e] = True
            tok = (e, self.cnt[e] + 1)
        self._record(tok, reads, writes)
        return tok

    def dma(self, q, out, in_, semkey, reads=(), writes=(), defer=False, **kw):
        waits = self._collect(q, reads, writes)
        self._emit_waits(q, waits)
        ins = self.eng[q].dma_start(out=out, in_=in_, **kw)
        self.cnt[semkey] += 16
        ins.then_inc(self.sems[semkey], 16)
        tok = (semkey, self.cnt[semkey])
        if not defer:
            self._record(tok, reads, writes)
        return tok

    def wait_tok(self, e, tok):
        waits = {}
        self._need(e, tok, waits)
        self._emit_waits(e, waits)


class PsumBank:
    def __init__(self, t, idx):
        self.t = t
        self.idx = idx
        self.buf = Buf()
        self.ap = t[:, idx * 512:(idx + 1) * 512]
        self.bf = self.ap.bitcast(BF16)


def host_consts():
    c = {}
    c["c_identf"] = np.eye(128, dtype=np.float32)
    cb = np.zeros((128, 8 * 128), np.float32)
    j = np.arange(128)[:, None]
    s = np.arange(128)[None, :]
    cb[:, CB_ID:CB_ID + 128] = np.eye(128)
    cb[:, CB_ONES:CB_ONES + 128] = 1.0
    cb[:, CB_BD:CB_BD + 128] = ((j // 64) == (s // 64)).astype(np.float32)
    sel = np.zeros((128, 128), np.float32)
    sel[0, :] = 1.0
    sel[32, :] = 1.0
    cb[:, CB_SEL:CB_SEL + 128] = sel
    cb[:, CB_NEGU:CB_NEGU + 128] = -(j >= s).astype(np.float32)
    cb[:, CB_NEGONES:CB_NEGONES + 128] = -1.0
    cb[:, CB_DMASK:CB_DMASK + 128] = np.where(s <= j, -30000.0, 0.0)
    c["c_b16"] = cb
    m = np.arange(128, dtype=np.float64)[:, None]
    n = np.arange(512, dtype=np.float64)[None, :]
    rm = np.zeros((128, RET_H * 512), np.float64)
    qd = np.zeros((128, RET_H * 512), np.float64)
    kd = np.zeros((128, 16), np.float64)
    for h, g in enumerate(GAMMAS):
        lg = math.log(g)
        rm[:, h * 512:(h + 1) * 512] = np.where(n >= m, np.exp(np.maximum(n - m, 0.0) * lg), 0.0) / 16.0
        qd[:, h * 512:(h + 1) * 512] = np.exp((n + 1.0) * lg)
        for jj in range(4):
            kd[:, h * 4 + jj] = np.exp((511.0 - (128.0 * jj + m[:, 0])) * lg) / 16.0
    c["c_retmask"] = rm.astype(np.float32)
    c["c_qdec"] = qd.astype(np.float32)
    c["c_kdec"] = kd.astype(np.float32)
    inv = 10000.0 ** (-np.arange(128, dtype=np.float32) / np.float32(128.0))
    c["c_invf"] = np.stack([inv.astype(np.float32), np.zeros(128, np.float32)], axis=1).astype(np.float32)
    return c


W_SHAPES = {
    "ada_w": [2, D, 6 * D], "ada_b": [2, 6 * D], "norm_mix_g": [2, D], "norm_ffn_g": [2, D],
    "ret_w_in": [1, D, 6 * D], "ret_w_out": [1, 2 * D, D], "kv_ada_w": [D, 2 * D], "kv_ada_b": [2 * D],
    "kv_norm_g": [D], "w_kv": [D, 2 * D], "k_norm_g": [64], "sb_w_q": [1, D, D], "q_norm_g": [1, 64],
    "sb_w_out": [1, D, D], "ffn_w_in": [2, D, 2 * FF], "ffn_conv_w": [2, 3, 2 * FF],
    "ffn_conv_b": [2, 2 * FF], "ffn_w_out": [2, FF, D],
}
C_SHAPES = {"c_identf": [128, 128], "c_b16": [128, 1024], "c_retmask": [128, 2048],
            "c_qdec": [128, 2048], "c_kdec": [128, 16], "c_invf": [128, 2]}


class Builder:
    def __init__(self, nseq=2, S=2048, stage="full"):
        self.nseq = nseq
        self.S = S
        self.NB = S // T
        self.stage = stage
        self.nc = bass.Bass("TRN2", target_bir_lowering=False)
        self.es = contextlib.ExitStack()

    def build(self):
        nc = self.nc
        with self.es:
            self._declare()
            self._prologue()
            for seq in range(self.nseq):
                self._seq_init(seq)
                for blk in range(self.NB):
                    self._block(seq, blk)
            self._finish()
        return nc

    def sb(self, name, shape, dt):
        return self.es.enter_context(self.nc.sbuf_tensor(name, shape, dt))

    def _declare(self):
        nc, es = self.nc, self.es
        nseq, S = self.nseq, self.S
        dr = {}
        dr["x"] = nc.dram_tensor("x", [nseq, S, D], F32, kind="ExternalInput").ap()
        dr["c"] = nc.dram_tensor("c", [nseq, D], F32, kind="ExternalInput").ap()
        dr["positions"] = nc.dram_tensor("positions", [nseq, S], I32, kind="ExternalInput").ap()
        for k, shp in W_SHAPES.items():
            dr[k] = nc.dram_tensor(k, shp, F32, kind="ExternalInput").ap()
        for k, shp in C_SHAPES.items():
            dr[k] = nc.dram_tensor(k, shp, F32, kind="ExternalInput").ap()
        dr["out"] = nc.dram_tensor("out", [nseq, S, D], F32, kind="ExternalOutput").ap()
        self.dr = dr
        self.S_ = Sched(nc, es)
        Sx = self.S_
        self.wtiles = self._wtile_list()
        self.NWT = len(self.wtiles)
        self.wscr = nc.dram_tensor("wscr", [self.NWT, 128, 4096], BF16).ap()
        self.scr_buf = [Buf() for _ in range(self.NWT)]
        self.identF = self.sb("identF", [128, 128], F32)
        self.cb16 = self.sb("cb16", [128, 1024], BF16)
        self.retmask = self.sb("retmask", [128, 2048], BF16)
        self.qdec = self.sb("qdec", [128, 2048], BF16)
        self.kdec = self.sb("kdec", [128, 16], F32)
        self.invf = self.sb("invf", [128, 2], F32)
        self.invf2 = self.sb("invf2", [128, 2], F32)
        self.vecT = self.sb("vecT", [128, 640], F32)
        self.mods = self.sb("mods", [128, 224], F32)
        self.gsc = self.sb("gsc", [128, 80], F32)
        self.qg8 = self.sb("qg8", [128, 2], F32)
        self.cact = self.sb("cact", [128, 16], BF16)
        self.h = self.sb("h", [128, 8 * T], F32)
        self.hn = self.sb("hn", [128, 8 * T], BF16)
        self.KT = self.sb("KT", [128, 8 * S], BF16)
        self.V = self.sb("V", [128, (S // 128) * D], BF16)
        self.state = self.sb("state", [128, 8 * 512], F32)
        self.halo = self.sb("halo", [128, 2 * 44 * 2], F32)
        self.gn = self.sb("gn", [128, 8], F32)
        self.wring = self.sb("wring", [128, NSLOT * 4096], BF16)
        self.fpool = self.sb("fpool", [128, NF * 520], F32)
        self.c2 = self.fpool[:, 8 * 520:8 * 520 + D]
        self.bpool = self.sb("bpool", [128, NBF * 512], BF16)
        self.cbuf = Buf()
        self.hb = [Buf() for _ in range(8)]
        self.hnb = [Buf() for _ in range(8)]
        self.KTb = [Buf() for _ in range(8)]
        self.Vb = [Buf() for _ in range(S // 128)]
        self.stb = [Buf() for _ in range(8)]
        self.halob = Buf()
        self.gnb = Buf()
        self.wslot_buf = [Buf() for _ in range(NSLOT)]
        self.fpb = [Buf() for _ in range(NF)]
        self.ubh = [Buf() for _ in range(NF)]
        self.bpb = [Buf() for _ in range(NBF)]
        for i in range(NSLOT):
            Sx.new_sem(f"w{i}")
            Sx.new_sem(f"ws{i}")
        for i in range(NF):
            Sx.new_sem(f"f{i}")
        for i in range(7, 15):
            Sx.new_sem(f"x{i}")
        for i in range(NCVT):
            Sx.new_sem(f"cv{i}")
        Sx.new_sem("cst")
        Sx.new_sem("cstp")
        self.cvt_last = [None] * NCVT
        self.wb_last = [None] * 4
        for i in range(4):
            Sx.new_sem(f"wb{i}")
        self.banks = []
        self.pst = es.enter_context(nc.psum_tensor("pst", [128, 4096], F32))
        for i in range(8):
            self.banks.append(PsumBank(self.pst, i))
        self.free_banks = list(range(8))
        self.wnext = 0
        self.wissued = 0
        self.wtotal = None

    def fp(self, i):
        return self.fpool[:, i * 520:(i + 1) * 520]

    def bp(self, i, n=1):
        return self.bpool[:, i * 512:(i + n) * 512]

    def hc(self, c):
        return self.h[:, c * T:(c + 1) * T]

    def hnc(self, c):
        return self.hn[:, c * T:(c + 1) * T]

    def cb(self, off):
        return self.cb16[:, off:off + 128]

    def vcol(self, col):
        return self.vecT[:, col:col + 1]

    def mcol(self, chunk, b):
        return self.mods[:, chunk * 2 + b:chunk * 2 + b + 1]

    def ps_alloc(self):
        assert self.free_banks, "out of PSUM banks"
        i = self.free_banks.pop(0)
        return self.banks[i]

    def ps_take(self, i):
        assert i in self.free_banks, f"bank {i} not free"
        self.free_banks.remove(i)
        return self.banks[i]

    def ps_free(self, bank):
        assert bank.idx not in self.free_banks
        self.free_banks.append(bank.idx)

    def _wtile_list(self):
        dr = self.dr

        def kp(ap2d):
            return ap2d.rearrange("(k p) e -> p k e", p=128)

        tl = []
        rin = kp(dr["ret_w_in"][0])
        rout = kp(dr["ret_w_out"][0])
        for h in range(RET_H):
            tl.append(dict(nrc=8, ncols=512, srcs=[(rin[:, :, h * 256:(h + 1) * 256], 0, 256),
                                                    (rin[:, :, D + h * 256:D + (h + 1) * 256], 256, 256)]))
            tl.append(dict(nrc=8, ncols=512, srcs=[(rin[:, :, 2 * D + h * 512:2 * D + (h + 1) * 512], 0, 512)]))
            tl.append(dict(nrc=8, ncols=512, srcs=[(rin[:, :, 4 * D + h * 512:4 * D + (h + 1) * 512], 0, 512)]))
            tl.append(dict(nrc=4, ncols=1024, srcs=[(rout[:, h * 4:(h + 1) * 4, :], 0, 1024)]))

        def ffn(l):
            win = kp(dr["ffn_w_in"][l])
            wout = kp(dr["ffn_w_out"][l])
            for j in range(11):
                tl.append(dict(nrc=8, ncols=512, srcs=[(win[:, :, j * 256:(j + 1) * 256], 0, 256),
                                                        (win[:, :, FF + j * 256:FF + (j + 1) * 256], 256, 256)]))
            for e in range(8):
                tl.append(dict(nrc=22, ncols=128, srcs=[(wout[:, :, e * 128:(e + 1) * 128], 0, 128)]))

        ffn(0)
        wkv = kp(dr["w_kv"])
        for j in range(4):
            tl.append(dict(nrc=8, ncols=512, srcs=[(wkv[:, :, j * 512:(j + 1) * 512], 0, 512)]))
        wq = kp(dr["sb_w_q"][0])
        for j in range(2):
            tl.append(dict(nrc=8, ncols=512, srcs=[(wq[:, :, j * 512:(j + 1) * 512], 0, 512)]))
        wo = kp(dr["sb_w_out"][0])
        for j in range(2):
            tl.append(dict(nrc=8, ncols=512, srcs=[(wo[:, :, j * 512:(j + 1) * 512], 0, 512)]))
        ffn(1)
        return tl

    def _convert_weights(self):
        Sx = self.S_
        for t, d in enumerate(self.wtiles):
            key = f"cv{t % NCVT}"
            if self.cvt_last[t % NCVT] is not None:
                Sx.wait_tok("pool", self.cvt_last[t % NCVT])
            n = d["nrc"] * d["ncols"]
            dst = self.wscr[t][:, 0:n].rearrange("p (k e) -> p k e", e=d["ncols"])
            tok = None
            for (src, off, w) in d["srcs"]:
                tok = Sx.dma("pool", dst[:, :, off:off + w], src, key, defer=True)
            self.cvt_last[t % NCVT] = tok
            self.scr_buf[t].w = tok

    def _w_issue(self):
        if self.wissued >= self.wtotal:
            return
        Sx = self.S_
        idx = self.wissued
        t = idx % self.NWT
        slot = idx % NSLOT
        d = self.wtiles[t]
        n = d["nrc"] * d["ncols"]
        ring = self.wring[:, slot * 4096:slot * 4096 + n]
        if idx < self.NWT:
            rv = ring.rearrange("p (k e) -> p k e", e=d["ncols"])
            for (src, off, w) in d["srcs"]:
                Sx.dma("pool", rv[:, :, off:off + w], src, f"ws{slot}", writes=[self.wslot_buf[slot]])
            key = f"wb{t % 4}"
            if self.wb_last[t % 4] is not None:
                Sx.wait_tok("sp", self.wb_last[t % 4])
            tok = Sx.dma("sp", self.wscr[t][:, 0:n], ring, key, reads=[self.wslot_buf[slot]],
                         writes=[self.scr_buf[t]])
            self.wb_last[t % 4] = tok
        else:
            Sx.dma("sp", ring, self.wscr[t][:, 0:n], f"w{slot}",
                   reads=[self.scr_buf[t]], writes=[self.wslot_buf[slot]])
        self.wissued += 1

    def w_get(self):
        idx = self.wnext
        assert idx < self.wissued
        slot = idx % NSLOT
        d = self.wtiles[idx % self.NWT]
        n = d["nrc"] * d["ncols"]
        v = self.wring[:, slot * 4096:slot * 4096 + n].rearrange("p (k e) -> p k e", e=d["ncols"])
        return v, self.wslot_buf[slot]

    def w_done(self):
        self.wnext += 1
        self._w_issue()

    def _prologue(self):
        Sx, dr = self.S_, self.dr
        cst_bufs = []

        def cload(q, out, in_, **kw):
            Sx.dma(q, out, in_, "cst" if q == "sp" else "cstp", defer=True, **kw)

        cload("sp", self.identF[:], dr["c_identf"])
        cload("pool", self.retmask[:], dr["c_retmask"])
        cload("sp", self.kdec[:], dr["c_kdec"])
        cload("sp", self.invf[:], dr["c_invf"])
        cload("sp", self.c2[0:self.nseq, :], dr["c"])
        cload("pool", self.cb16[:], dr["c_b16"])
        cload("pool", self.qdec[:], dr["c_qdec"])
        rows = []
        rows.append((dr["ada_b"].rearrange("l (c p) -> (l c) p", p=128), 96))
        rows.append((dr["kv_ada_b"].rearrange("(c p) -> c p", p=128), 16))
        rows.append((dr["norm_mix_g"].rearrange("l (c p) -> (l c) p", p=128), 16))
        rows.append((dr["norm_ffn_g"].rearrange("l (c p) -> (l c) p", p=128), 16))
        rows.append((dr["kv_norm_g"].rearrange("(c p) -> c p", p=128), 8))
        rows.append((dr["ffn_conv_b"].rearrange("l (c p) -> (l c) p", p=128), 88))
        rows.append(("kg", 1))
        rows.append(("qg", 1))
        rows.append(("pad", 14))
        rows.append((dr["ffn_conv_w"].rearrange("l t (c p) -> (l t c) p", p=128), 264))
        stage_tiles = [3, 4, 5, 6, 7]
        for g in stage_tiles:
            Sx.op("dve", lambda e, g=g: e.memset(self.fp(g)[:, 0:128], 0.0), writes=[self.fpb[g]])
        Sx.wait_tok("sp", ("dve", Sx.cnt["dve"]))
        r = 0
        for (src, n) in rows:
            if isinstance(src, str):
                if src == "kg":
                    for half in range(2):
                        cload("sp", self.fp(3 + r // 128)[r % 128:r % 128 + 1, half * 64:(half + 1) * 64],
                              dr["k_norm_g"].rearrange("(o d) -> o d", o=1))
                elif src == "qg":
                    for half in range(2):
                        cload("sp", self.fp(3 + r // 128)[r % 128:r % 128 + 1, half * 64:(half + 1) * 64],
                              dr["q_norm_g"])
                r += n
                continue
            done = 0
            while done < n:
                g = r // 128
                p0 = r % 128
                take = min(n - done, 128 - p0)
                cload("sp", self.fp(3 + g)[p0:p0 + take, 0:128], src[done:done + take, :])
                done += take
                r += take
        assert r == 520, r
        tok = ("cst", Sx.cnt["cst"])
        for e_ in ("pe", "act", "dve", "pool"):
            Sx.wait_tok(e_, ("cstp", Sx.cnt["cstp"]))
        self.cbuf.w = tok
        for g in stage_tiles + [8, 9]:
            self.fpb[g].w = tok
        pa = self.ps_alloc()
        pb = self.ps_alloc()
        for g in range(5):
            nr = 128 if g < 4 else 8
            bank = pa if g < 4 else pb
            col = (g % 4) * 128
            Sx.op("pe", lambda e, g=g, nr=nr, bank=bank, col=col: e.transpose(
                bank.ap[:, col:col + nr], self.fp(3 + g)[0:nr, 0:128], self.identF[0:nr, 0:nr]),
                reads=[self.fpb[3 + g], self.cbuf], writes=[bank.buf])
        Sx.op("dve", lambda e: e.tensor_copy(self.vecT[:, 0:512], pa.ap[:, 0:512]), reads=[pa.buf], writes=[self.cbuf])
        Sx.op("dve", lambda e: e.tensor_copy(self.vecT[:, 512:520], pb.ap[:, 0:8]), reads=[pb.buf], writes=[self.cbuf])
        self.ps_free(pa)
        self.ps_free(pb)
        Sx.op("dve", lambda e: e.tensor_scalar_mul(self.invf2[:], self.invf[:], 1.0 / TWO_PI),
              reads=[self.cbuf], writes=[self.cbuf])
        pc = self.ps_alloc()
        for k in range(8):
            Sx.op("pe", lambda e, k=k: e.transpose(pc.ap[:, k * 2:k * 2 + self.nseq],
                                                   self.c2[0:self.nseq, k * 128:(k + 1) * 128],
                                                   self.identF[0:self.nseq, 0:self.nseq]),
                  reads=[self.cbuf, self.fpb[8], self.fpb[9]], writes=[pc.buf])
        if self.nseq < 2:
            Sx.op("dve", lambda e: e.memset(self.cact[:], 0.0), writes=[self.cbuf])
        for k in range(8):
            Sx.op("act", lambda e, k=k: e.activation(out=self.cact[:, k * 2:k * 2 + self.nseq],
                                                     in_=pc.ap[:, k * 2:k * 2 + self.nseq], func=AF.Silu),
                  reads=[pc.buf], writes=[self.cbuf])
        self.ps_free(pc)
        ada_tiles = []
        for l in range(2):
            wl = dr["ada_w"][l].rearrange("(k p) e -> p k e", p=128)
            for j in range(12):
                ada_tiles.append((wl[:, :, j * 512:(j + 1) * 512], l * 48 + j * 4, VC_ADAB + l * 48 + j * 4))
        wk = dr["kv_ada_w"].rearrange("(k p) e -> p k e", p=128)
        for j in range(4):
            ada_tiles.append((wk[:, :, j * 512:(j + 1) * 512], 96 + j * 4, VC_ADAB + 96 + j * 4))
        nada = len(ada_tiles)

        def ada_issue(i):
            if i >= nada:
                return
            slot = i % NSLOT
            Sx.dma("pool", self.wring[:, slot * 4096:(slot + 1) * 4096].rearrange("p (k e) -> p k e", e=512),
                   ada_tiles[i][0], f"ws{slot}", writes=[self.wslot_buf[slot]])

        for i in range(NSLOT):
            ada_issue(i)
        for i, (_, ch0, vc0) in enumerate(ada_tiles):
            slot = i % NSLOT
            wv = self.wring[:, slot * 4096:(slot + 1) * 4096].rearrange("p (k e) -> p k e", e=512)
            pm = self.ps_alloc()
            for cc in range(4):
                for k in range(8):
                    Sx.op("pe", lambda e, cc=cc, k=k: e.matmul(pm.ap[:, cc * 2:cc * 2 + 2],
                                                               wv[:, k, cc * 128:(cc + 1) * 128],
                                                               self.cact[:, k * 2:k * 2 + 2],
                                                               start=(k == 0), stop=(k == 7)),
                          reads=[self.wslot_buf[slot], self.cbuf], writes=[pm.buf], signal=(k == 7))
            for cc in range(4):
                Sx.op("dve", lambda e, cc=cc: e.tensor_scalar_add(
                    self.mods[:, (ch0 + cc) * 2:(ch0 + cc) * 2 + 2], pm.ap[:, cc * 2:cc * 2 + 2],
                    self.vcol(vc0 + cc)), reads=[pm.buf, self.cbuf], writes=[self.cbuf])
            self.ps_free(pm)
            ada_issue(i + NSLOT)
        norm_defs = [(VC_NMG + 0, 8), (VC_NFG + 0, 32), (VC_KVG, 104), (VC_NMG + 8, 48 + 8), (VC_NFG + 8, 48 + 32)]
        self.norm_shift = [0, 24, 96, 48 + 0, 48 + 24]
        for n, (gcol, sc0) in enumerate(norm_defs):
            for c in range(8):
                gv = self.gsc[:, (n * 8 + c) * 2:(n * 8 + c) * 2 + 2]
                Sx.op("dve", lambda e, gv=gv, c=c, sc0=sc0: e.tensor_scalar_add(
                    gv, self.mods[:, (sc0 + c) * 2:(sc0 + c) * 2 + 2], 1.0), reads=[self.cbuf], writes=[self.cbuf])
                Sx.op("dve", lambda e, gv=gv, c=c, gcol=gcol: e.tensor_scalar_mul(gv, gv, self.vcol(gcol + c)),
                      reads=[self.cbuf], writes=[self.cbuf])
        self.wtotal = self.nseq * self.NB * self.NWT
        for _ in range(NSLOT):
            self._w_issue()

    def _seq_init(self, seq):
        Sx = self.S_
        for i in range(8):
            Sx.op("pool", lambda e, i=i: e.memset(self.state[:, i * 512:(i + 1) * 512], 0.0), writes=[self.stb[i]])
        Sx.op("pool", lambda e: e.memset(self.halo[:], 0.0), writes=[self.halob])

    def load_x(self, seq, blk):
        Sx, dr = self.S_, self.dr
        t0 = blk * T
        stg = []
        for i in range(4):
            for half in range(2):
                if i < 2:
                    fi = 7 + i * 2 + half
                    stg.append((self.fp(fi)[:, 0:512], [self.fpb[fi]], f"x{fi}"))
                else:
                    k = (i - 2) * 2 + half
                    bi = 24 + 2 * k
                    stg.append((self.bp(bi, 2).bitcast(F32), [self.bpb[bi], self.bpb[bi + 1]], f"x{11 + k}"))
        for i in range(4):
            for half in range(2):
                ap, bufs, key = stg[i * 2 + half]
                Sx.dma("pool", ap, dr["x"][seq, t0 + i * 128:t0 + (i + 1) * 128, half * 512:(half + 1) * 512],
                       key, writes=bufs)
        for i in range(4):
            for half in range(2):
                ap, bufs, key = stg[i * 2 + half]
                ps = self.ps_alloc()
                for cq in range(4):
                    Sx.op("pe", lambda e, cq=cq, ap=ap, ps=ps: e.transpose(
                        ps.ap[:, cq * 128:(cq + 1) * 128], ap[:, cq * 128:(cq + 1) * 128], self.identF[:]),
                        reads=bufs + [self.cbuf], writes=[ps.buf], signal=(cq == 3))
                hv = self.h[:, :].rearrange("p (c t) -> p c t", t=T)[:, half * 4:(half + 1) * 4, i * 128:(i + 1) * 128]
                pv = ps.ap.rearrange("p (c t) -> p c t", t=128)
                if half == 0:
                    Sx.op("act", lambda e, hv=hv, pv=pv: e.activation(out=hv, in_=pv, func=AF.Copy),
                          reads=[ps.buf], writes=self.hb[half * 4:(half + 1) * 4])
                else:
                    Sx.op("dve", lambda e, hv=hv, pv=pv: e.tensor_copy(hv, pv),
                          reads=[ps.buf], writes=self.hb[half * 4:(half + 1) * 4])
                self.ps_free(ps)

    def store_out(self, seq, blk):
        Sx, dr = self.S_, self.dr
        t0 = blk * T
        for i in range(4):
            for half in range(2):
                fi = 3 + (i % 2) * 2 + half
                ps = self.ps_alloc()
                for cq in range(4):
                    c = half * 4 + cq
                    Sx.op("pe", lambda e, cq=cq, c=c, ps=ps: e.transpose(
                        ps.ap[:, cq * 128:(cq + 1) * 128], self.hc(c)[:, i * 128:(i + 1) * 128], self.identF[:]),
                        reads=[self.hb[c], self.cbuf], writes=[ps.buf], signal=(cq == 3))
                if half == 0:
                    Sx.op("act", lambda e, fi=fi, ps=ps: e.activation(out=self.fp(fi)[:, 0:512], in_=ps.ap, func=AF.Copy),
                          reads=[ps.buf], writes=[self.fpb[fi]])
                else:
                    Sx.op("dve", lambda e, fi=fi, ps=ps: e.tensor_copy(self.fp(fi)[:, 0:512], ps.ap),
                          reads=[ps.buf], writes=[self.fpb[fi]])
                self.ps_free(ps)
                Sx.dma("sp", dr["out"][seq, t0 + i * 128:t0 + (i + 1) * 128, half * 512:(half + 1) * 512],
                       self.fp(fi)[:, 0:512], f"f{fi}", reads=[self.fpb[fi]])

    def norm(self, nidx, b, reuse_rstd=False):
        Sx = self.S_
        if reuse_rstd:
            self._norm_apply(nidx, b)
            return
        ps = self.ps_alloc()
        for c in range(8):
            si = c % 2
            if c % 3 == 2:
                Sx.op("dve", lambda e, c=c, si=si: e.tensor_tensor(self.bp(si), self.hc(c), self.hc(c), op=ALU.mult),
                      reads=[self.hb[c]], writes=[self.bpb[si]])
            else:
                Sx.op("act", lambda e, c=c, si=si: e.activation(out=self.bp(si), in_=self.hc(c), func=AF.Square),
                      reads=[self.hb[c]], writes=[self.bpb[si]])
            Sx.op("pe", lambda e, c=c, si=si: e.matmul(ps.ap, self.cb(CB_ONES), self.bp(si), start=(c == 0), stop=(c == 7)),
                  reads=[self.bpb[si], self.cbuf], writes=[ps.buf])
        rstd = self.fp(0)[:, 0:512]
        Sx.op("act", lambda e: e.activation(out=rstd, in_=ps.ap, func=AF.Ln, scale=1.0 / D, bias=EPS),
              reads=[ps.buf], writes=[self.fpb[0]])
        self.ps_free(ps)
        Sx.op("act", lambda e: e.activation(out=rstd, in_=rstd, func=AF.Exp, scale=-0.5),
              reads=[self.fpb[0]], writes=[self.fpb[0]])
        self._norm_apply(nidx, b)

    def _norm_apply(self, nidx, b):
        Sx = self.S_
        rstd = self.fp(0)[:, 0:512]
        sh0 = self.norm_shift[nidx]
        for c in range(8):
            ti = 1 + c % 2
            gcol = (nidx * 8 + c) * 2 + b
            Sx.op("dve", lambda e, c=c, ti=ti: e.tensor_tensor(self.fp(ti)[:, 0:512], self.hc(c), rstd, op=ALU.mult),
                  reads=[self.hb[c], self.fpb[0]], writes=[self.fpb[ti]])
            if c % 3 == 2:
                Sx.op("pool", lambda e, c=c, ti=ti, gcol=gcol: e.tensor_scalar(
                    self.hnc(c), self.fp(ti)[:, 0:512], self.gsc[:, gcol:gcol + 1], self.mcol(sh0 + c, b),
                    op0=ALU.mult, op1=ALU.add),
                    reads=[self.fpb[ti], self.cbuf], writes=[self.hnb[c]])
            else:
                Sx.op("act", lambda e, c=c, ti=ti, gcol=gcol: e.activation(
                    out=self.hnc(c), in_=self.fp(ti)[:, 0:512], func=AF.Identity,
                    scale=self.gsc[:, gcol:gcol + 1], bias=self.mcol(sh0 + c, b)),
                    reads=[self.fpb[ti], self.cbuf], writes=[self.hnb[c]])

    def proj_fm(self, wv, wb, col0, ps):
        Sx = self.S_
        for k in range(8):
            Sx.op("pe", lambda e, k=k: e.matmul(ps.ap, wv[:, k, col0:col0 + 128], self.hnc(k),
                                                start=(k == 0), stop=(k == 7)),
                  reads=[wb, self.hnb[k]], writes=[ps.buf], signal=(k == 7))

    def proj_fm_multi(self, wv, wb, cols, pss):
        Sx = self.S_
        for k in range(8):
            for col0, ps in zip(cols, pss):
                Sx.op("pe", lambda e, k=k, col0=col0, ps=ps: e.matmul(ps.ap, wv[:, k, col0:col0 + 128], self.hnc(k),
                                                                     start=(k == 0), stop=(k == 7)),
                      reads=[wb, self.hnb[k]], writes=[ps.buf], signal=(k == 7))

    def proj_tm(self, wv, wb, i, ps, col0=0):
        Sx = self.S_
        for k in range(8):
            Sx.op("pe", lambda e, k=k: e.matmul(ps.ap, self.hnc(k)[:, i * 128:(i + 1) * 128], wv[:, k, col0:col0 + 512],
                                                start=(k == 0), stop=(k == 7)),
                  reads=[wb, self.hnb[k]], writes=[ps.buf], signal=(k == 7))

    def rope_tables(self, seq, blk):
        Sx, dr = self.S_, self.dr
        t0 = blk * T
        posi = self.fp(5)[:, 0:512].bitcast(I32)
        Sx.dma("sp", posi, dr["positions"][seq:seq + 1, t0:t0 + T].partition_broadcast(128), "f5",
               writes=[self.fpb[5]])
        posf = self.fp(6)[:, 0:512]
        Sx.op("dve", lambda e: e.tensor_copy(posf, posi), reads=[self.fpb[5]], writes=[self.fpb[6]])
        ang = self.fp(5)[:, 0:512]
        Sx.op("dve", lambda e: e.tensor_scalar_mul(ang, posf, self.invf[:, 0:1]),
              reads=[self.fpb[6], self.cbuf], writes=[self.fpb[5]])
        for which, dst in (("sin", 4), ("cos", 3)):
            a = ang
            if which == "cos":
                a = self.fp(6)[:, 0:512]
                Sx.op("dve", lambda e, a=a: e.tensor_scalar_add(a, ang, math.pi / 2.0),
                      reads=[self.fpb[5]], writes=[self.fpb[6]])
                ab = self.fpb[6]
            else:
                ab = self.fpb[5]
            kk = self.fp(7)[:, 0:512]
            Sx.op("dve", lambda e, a=a: e.tensor_scalar(kk, a, 1.0 / TWO_PI, MAGIC, op0=ALU.mult, op1=ALU.add),
                  reads=[ab], writes=[self.fpb[7]])
            Sx.op("dve", lambda e: e.tensor_scalar_add(kk, kk, -MAGIC), reads=[self.fpb[7]], writes=[self.fpb[7]])
            r = self.fp(8)[:, 0:512]
            Sx.op("dve", lambda e, a=a: e.scalar_tensor_tensor(out=r, in0=kk, scalar=-TWO_PI, in1=a,
                                                               op0=ALU.mult, op1=ALU.add),
                  reads=[self.fpb[7], ab], writes=[self.fpb[8]])
            Sx.op("dve", lambda e: e.tensor_scalar(r, r, -math.pi, math.pi, op0=ALU.max, op1=ALU.min),
                  reads=[self.fpb[8]], writes=[self.fpb[8]])
            Sx.op("act", lambda e, dst=dst: e.activation(out=self.fp(dst)[:, 0:512], in_=r, func=AF.Sin),
                  reads=[self.fpb[8]], writes=[self.fpb[dst]])

    def retention(self, seq, blk):
        Sx = self.S_
        b = seq
        self.norm(0, b)
        self.rope_tables(seq, blk)
        cos = self.fp(3)[:, 0:512]
        sin = self.fp(4)[:, 0:512]
        BQT, BQD, BKT, BKD, BV, BSG, BST, BON, BONT, BSB = 2, 4, 6, 8, 10, 14, 18, 22, 26, 30
        for hd in range(RET_H):
            wv, wb = self.w_get()
            pq = [self.ps_alloc() for _ in range(4)]
            if hd == 0:
                self.proj_fm_multi(wv, wb, [cc * 128 for cc in range(4)], pq)
            else:
                for cc in range(4):
                    self.proj_fm(wv, wb, cc * 128, pq[cc])
            self.w_done()
            wv, wb = self.w_get()
            for i in range(4):
                ps = self.ps_alloc()
                self.proj_tm(wv, wb, i, ps)
                Sx.op("act", lambda e, i=i, ps=ps: e.activation(out=self.bp(BV + i), in_=ps.ap, func=AF.Copy),
                      reads=[ps.buf], writes=[self.bpb[BV + i]])
                self.ps_free(ps)
            self.w_done()
            wv, wb = self.w_get()
            for i in range(4):
                ps = self.ps_alloc()
                self.proj_tm(wv, wb, i, ps)
                Sx.op("act", lambda e, i=i, ps=ps: e.activation(out=self.bp(BSG + i), in_=ps.ap, func=AF.Silu),
                      reads=[ps.buf], writes=[self.bpb[BSG + i]])
                self.ps_free(ps)
            self.w_done()
            for (p_lo, p_hi, dst) in ((pq[0], pq[1], BQT), (pq[2], pq[3], BKT)):
                Sx.op("dve", lambda e, p=p_lo: e.tensor_tensor(self.fp(5)[:, 0:512], p.ap, cos, op=ALU.mult),
                      reads=[p_lo.buf, self.fpb[3]], writes=[self.fpb[5]])
                Sx.op("dve", lambda e, p=p_hi: e.tensor_tensor(self.fp(6)[:, 0:512], p.ap, sin, op=ALU.mult),
                      reads=[p_hi.buf, self.fpb[4]], writes=[self.fpb[6]])
                Sx.op("pool", lambda e, dst=dst: e.tensor_tensor(self.bp(dst), self.fp(5)[:, 0:512], self.fp(6)[:, 0:512],
                                                                  op=ALU.subtract),
                      reads=[self.fpb[5], self.fpb[6]], writes=[self.bpb[dst]])
                Sx.op("dve", lambda e, p=p_lo: e.tensor_tensor(self.fp(7)[:, 0:512], p.ap, sin, op=ALU.mult),
                      reads=[p_lo.buf, self.fpb[4]], writes=[self.fpb[7]])
                Sx.op("dve", lambda e, p=p_hi: e.tensor_tensor(self.fp(8)[:, 0:512], p.ap, cos, op=ALU.mult),
                      reads=[p_hi.buf, self.fpb[3]], writes=[self.fpb[8]])
                Sx.op("pool", lambda e, dst=dst: e.tensor_tensor(self.bp(dst + 1), self.fp(7)[:, 0:512], self.fp(8)[:, 0:512],
                                                                  op=ALU.add),
                      reads=[self.fpb[7], self.fpb[8]], writes=[self.bpb[dst + 1]])
            for p in pq:
                self.ps_free(p)
            for dc in range(2):
                Sx.op("pool", lambda e, dc=dc: e.tensor_tensor(self.bp(BQD + dc), self.bp(BQT + dc),
                                                                 self.qdec[:, hd * 512:(hd + 1) * 512], op=ALU.mult),
                      reads=[self.bpb[BQT + dc], self.cbuf], writes=[self.bpb[BQD + dc]])
            pt = self.ps_alloc()
            for i in range(4):
                for dc in range(2):
                    Sx.op("pe", lambda e, i=i, dc=dc: e.transpose(
                        pt.bf[:, (i * 2 + dc) * 128:(i * 2 + dc + 1) * 128],
                        self.bp(BKT + dc)[:, i * 128:(i + 1) * 128], self.cb(CB_ID)),
                        reads=[self.bpb[BKT + dc], self.cbuf], writes=[pt.buf])
            kd = self.bp(BKD, 2)
            for i in range(4):
                Sx.op("act", lambda e, i=i: e.activation(out=kd[:, i * 256:(i + 1) * 256], in_=pt.bf[:, i * 256:(i + 1) * 256],
                                                         func=AF.Identity, scale=self.kdec[:, hd * 4 + i:hd * 4 + i + 1]),
                      reads=[pt.buf, self.cbuf], writes=[self.bpb[BKD], self.bpb[BKD + 1]])
            self.ps_free(pt)
            for j in range(4):
                n = 512 - 128 * j
                ps = self.ps_alloc()
                for dc in range(2):
                    Sx.op("pe", lambda e, j=j, dc=dc, n=n, ps=ps: e.matmul(
                        ps.ap[:, 0:n], self.bp(BKT + dc)[:, j * 128:(j + 1) * 128], self.bp(BQT + dc)[:, j * 128:512],
                        start=(dc == 0), stop=(dc == 1)),
                        reads=[self.bpb[BKT + dc], self.bpb[BQT + dc]], writes=[ps.buf], signal=(dc == 1))
                Sx.op("dve", lambda e, j=j, n=n, ps=ps: e.tensor_tensor(
                    self.bp(BST + j)[:, 0:n], ps.ap[:, 0:n], self.retmask[:, hd * 512:hd * 512 + n], op=ALU.mult),
                    reads=[ps.buf, self.cbuf], writes=[self.bpb[BST + j]])
                self.ps_free(ps)
            for dc in range(2):
                Sx.op("act", lambda e, dc=dc: e.activation(out=self.bp(BSB + dc), in_=self.state[:, (hd * 2 + dc) * 512:(hd * 2 + dc + 1) * 512], func=AF.Copy),
                      reads=[self.stb[hd * 2 + dc]], writes=[self.bpb[BSB + dc]])
            po = [self.ps_alloc() for _ in range(4)]
            Sx.op("dve", lambda e: e.memset(self.gn[:], 0.0), writes=[self.gnb])
            for c in range(4):
                nmm = c + 1 + 2
                idx = 0
                for j in range(c + 1):
                    Sx.op("pe", lambda e, c=c, j=j, idx=idx, nmm=nmm: e.matmul(
                        po[c].ap, self.bp(BST + j)[:, (c - j) * 128:(c - j + 1) * 128], self.bp(BV + j),
                        start=(idx == 0), stop=False),
                        reads=[self.bpb[BST + j], self.bpb[BV + j]], writes=[po[c].buf], signal=False)
                    idx += 1
                for dc in range(2):
                    Sx.op("pe", lambda e, c=c, dc=dc: e.matmul(
                        po[c].ap, self.bp(BQD + dc)[:, c * 128:(c + 1) * 128], self.bp(BSB + dc),
                        start=False, stop=(dc == 1)),
                        reads=[self.bpb[BQD + dc], self.bpb[BSB + dc]], writes=[po[c].buf], signal=(dc == 1))
                Sx.op("act", lambda e, c=c: e.activation(out=self.bp(1), in_=po[c].ap, func=AF.Square,
                                                         accum_out=self.gn[:, c:c + 1]),
                      reads=[po[c].buf], writes=[self.bpb[1], self.gnb])
            Sx.op("act", lambda e: e.activation(out=self.gn[:, 4:8], in_=self.gn[:, 0:4], func=AF.Ln, scale=1.0 / 512.0, bias=EPS),
                  reads=[self.gnb], writes=[self.gnb])
            Sx.op("act", lambda e: e.activation(out=self.gn[:, 4:8], in_=self.gn[:, 4:8], func=AF.Exp, scale=-0.5),
                  reads=[self.gnb], writes=[self.gnb])
            for c in range(4):
                Sx.op("dve", lambda e, c=c: e.scalar_tensor_tensor(
                    out=self.bp(BON + c), in0=po[c].ap, scalar=self.gn[:, 4 + c:5 + c], in1=self.bp(BSG + c),
                    op0=ALU.mult, op1=ALU.mult),
                    reads=[po[c].buf, self.gnb, self.bpb[BSG + c]], writes=[self.bpb[BON + c]])
                self.ps_free(po[c])
            gC = GAMMAS[hd] ** 512
            for dc in range(2):
                ps = self.ps_alloc()
                for j in range(4):
                    Sx.op("pe", lambda e, j=j, dc=dc, ps=ps: e.matmul(
                        ps.ap, kd[:, j * 256 + dc * 128:j * 256 + (dc + 1) * 128], self.bp(BV + j),
                        start=(j == 0), stop=(j == 3)),
                        reads=[self.bpb[BKD], self.bpb[BKD + 1], self.bpb[BV + j]], writes=[ps.buf], signal=(j == 3))
                si = hd * 2 + dc
                st = self.state[:, si * 512:(si + 1) * 512]
                Sx.op("dve", lambda e, st=st, ps=ps: e.scalar_tensor_tensor(out=st, in0=st, scalar=gC, in1=ps.ap,
                                                                           op0=ALU.mult, op1=ALU.add),
                      reads=[ps.buf, self.stb[si]], writes=[self.stb[si]])
                self.ps_free(ps)
            for vc in range(4):
                pt = self.ps_alloc()
                for c in range(4):
                    Sx.op("pe", lambda e, vc=vc, c=c, pt=pt: e.transpose(
                        pt.bf[:, c * 128:(c + 1) * 128], self.bp(BON + c)[:, vc * 128:(vc + 1) * 128], self.cb(CB_ID)),
                        reads=[self.bpb[BON + c], self.cbuf], writes=[pt.buf], signal=(c == 3))
                if vc % 2 == 0:
                    Sx.op("act", lambda e, vc=vc, pt=pt: e.activation(out=self.bp(BONT + vc), in_=pt.bf[:, 0:512], func=AF.Copy),
                          reads=[pt.buf], writes=[self.bpb[BONT + vc]])
                else:
                    Sx.op("dve", lambda e, vc=vc, pt=pt: e.tensor_copy(self.bp(BONT + vc), pt.bf[:, 0:512]),
                          reads=[pt.buf], writes=[self.bpb[BONT + vc]])
                self.ps_free(pt)
            wv, wb = self.w_get()
            for ec in range(8):
                ps = self.ps_alloc()
                for vc in range(4):
                    Sx.op("pe", lambda e, ec=ec, vc=vc, ps=ps: e.matmul(
                        ps.ap, wv[:, vc, ec * 128:(ec + 1) * 128], self.bp(BONT + vc), start=(vc == 0), stop=(vc == 3)),
                        reads=[wb, self.bpb[BONT + vc]], writes=[ps.buf], signal=(vc == 3))
                Sx.op("dve", lambda e, ec=ec, ps=ps: e.scalar_tensor_tensor(
                    out=self.hc(ec), in0=ps.ap, scalar=self.mcol(16 + ec, b), in1=self.hc(ec),
                    op0=ALU.mult, op1=ALU.add), reads=[ps.buf, self.hb[ec], self.cbuf], writes=[self.hb[ec]])
                self.ps_free(ps)
            self.w_done()

    def ffn(self, l, seq, blk):
        Sx = self.S_
        b = seq
        self.norm(1 if l == 0 else 4, b)
        BACT = 2
        pend = None

        def finish(p):
            fc, yv, yvi, yg, ygi = p
            Sx.op("act", lambda e: e.activation(out=yg, in_=yg, func=AF.Silu),
                  reads=[self.fpb[ygi]], writes=[self.fpb[ygi]])
            Sx.op("dve", lambda e: e.tensor_tensor(self.bp(BACT + fc), yv, yg, op=ALU.mult),
                  reads=[self.fpb[yvi], self.fpb[ygi]], writes=[self.bpb[BACT + fc]])

        for j in range(11):
            wv, wb = self.w_get()
            pre = None
            if j == 0:
                pre = {(s_, g_): self.ps_alloc() for s_ in range(2) for g_ in range(2)}
                keys = list(pre.keys())
                self.proj_fm_multi(wv, wb, [g_ * 256 + s_ * 128 for (s_, g_) in keys], [pre[k_] for k_ in keys])
            for s in range(2):
                fc = 2 * j + s
                ys = []
                for gi in range(2):
                    ui_ = 3 + gi * 2 + (fc % 2)
                    hcol_ = (l * 44 + gi * 22 + fc) * 2
                    Sx.op("pool", lambda e, ui_=ui_, hcol_=hcol_: e.tensor_copy(self.fp(ui_)[:, 0:2], self.halo[:, hcol_:hcol_ + 2]),
                          reads=[self.halob], writes=[self.ubh[ui_]])
                for gi in range(2):
                    ci = gi * 22 + fc
                    if pre is not None:
                        ps = pre[(s, gi)]
                    else:
                        ps = self.ps_alloc()
                        self.proj_fm(wv, wb, gi * 256 + s * 128, ps)
                    ui = 3 + gi * 2 + (fc % 2)
                    yi = 7 + gi * 2 + (fc % 2)
                    ub = self.fp(ui)
                    y = self.fp(yi)[:, 0:512]
                    hcol = (l * 44 + ci) * 2
                    w0 = self.vcol(VC_CONVW + (l * 3 + 0) * 44 + ci)
                    w1 = self.vcol(VC_CONVW + (l * 3 + 1) * 44 + ci)
                    w2 = self.vcol(VC_CONVW + (l * 3 + 2) * 44 + ci)
                    bb = self.vcol(VC_CONVB + l * 44 + ci)
                    Sx.op("act", lambda e, ub=ub, ps=ps: e.activation(out=ub[:, 2:514], in_=ps.ap, func=AF.Copy),
                          reads=[ps.buf], writes=[self.fpb[ui]])
                    Sx.op("act", lambda e, y=y, ps=ps, w2=w2, bb=bb: e.activation(out=y, in_=ps.ap, func=AF.Identity,
                                                                                 scale=w2, bias=bb),
                          reads=[ps.buf, self.cbuf], writes=[self.fpb[yi]])
                    self.ps_free(ps)
                    Sx.op("dve", lambda e, y=y, ub=ub, w1=w1: e.scalar_tensor_tensor(
                        out=y, in0=ub[:, 1:513], scalar=w1, in1=y, op0=ALU.mult, op1=ALU.add),
                        reads=[self.fpb[ui], self.ubh[ui], self.fpb[yi], self.cbuf], writes=[self.fpb[yi]])
                    Sx.op("dve", lambda e, y=y, ub=ub, w0=w0: e.scalar_tensor_tensor(
                        out=y, in0=ub[:, 0:512], scalar=w0, in1=y, op0=ALU.mult, op1=ALU.add),
                        reads=[self.fpb[ui], self.ubh[ui], self.fpb[yi], self.cbuf], writes=[self.fpb[yi]])
                    Sx.op("pool", lambda e, ub=ub, hcol=hcol: e.tensor_copy(self.halo[:, hcol:hcol + 2], ub[:, 512:514]),
                          reads=[self.fpb[ui]], writes=[self.halob])
                    ys.append((y, yi))
                    if gi == 0 and pend is not None:
                        finish(pend)
                        pend = None
                pend = (fc, ys[0][0], ys[0][1], ys[1][0], ys[1][1])
            self.w_done()
        finish(pend)
        G = 3
        gt = []
        for g in range(G):
            idx = self.wnext + g
            assert idx < self.wissued
            slot = idx % NSLOT
            dsc = self.wtiles[idx % self.NWT]
            nn = dsc["nrc"] * dsc["ncols"]
            gt.append((self.wring[:, slot * 4096:slot * 4096 + nn].rearrange("p (k e) -> p k e", e=dsc["ncols"]),
                       self.wslot_buf[slot], self.ps_alloc()))
        for fc in range(NFC):
            for (gv, gb, gps) in gt:
                Sx.op("pe", lambda e, fc=fc, gv=gv, gps=gps: e.matmul(gps.ap, gv[:, fc, :], self.bp(BACT + fc),
                                                                      start=(fc == 0), stop=(fc == NFC - 1)),
                      reads=[gb, self.bpb[BACT + fc]], writes=[gps.buf], signal=(fc == NFC - 1))
        for ec, (gv, gb, gps) in enumerate(gt):
            self.w_done()
            Sx.op("dve", lambda e, ec=ec, gps=gps: e.scalar_tensor_tensor(
                out=self.hc(ec), in0=gps.ap, scalar=self.mcol(l * 48 + 40 + ec, b), in1=self.hc(ec),
                op0=ALU.mult, op1=ALU.add), reads=[gps.buf, self.hb[ec], self.cbuf], writes=[self.hb[ec]])
            self.ps_free(gps)
        for ec in range(G, 8):
            wv, wb = self.w_get()
            ps = self.ps_alloc()
            for fc in range(NFC):
                Sx.op("pe", lambda e, fc=fc, ps=ps: e.matmul(ps.ap, wv[:, fc, :], self.bp(BACT + fc),
                                                             start=(fc == 0), stop=(fc == NFC - 1)),
                      reads=[wb, self.bpb[BACT + fc]], writes=[ps.buf], signal=(fc == NFC - 1))
            self.w_done()
            Sx.op("dve", lambda e, ec=ec, ps=ps: e.scalar_tensor_tensor(
                out=self.hc(ec), in0=ps.ap, scalar=self.mcol(l * 48 + 40 + ec, b), in1=self.hc(ec),
                op0=ALU.mult, op1=ALU.add), reads=[ps.buf, self.hb[ec], self.cbuf], writes=[self.hb[ec]])
            self.ps_free(ps)

    def headnorm_a(self, ps, par):
        Sx = self.S_
        Sx.op("act", lambda e: e.activation(out=self.bp(par), in_=ps.ap, func=AF.Square),
              reads=[ps.buf], writes=[self.bpb[par]])

    def headnorm_b(self, ps, par, gain_col, dst_ap, dst_bufs, lnbias):
        Sx = self.S_
        p2 = self.ps_alloc()
        Sx.op("pe", lambda e: e.matmul(p2.ap, self.cb(CB_BD), self.bp(par), start=True, stop=True),
              reads=[self.bpb[par], self.cbuf], writes=[p2.buf])
        rs = self.fp(1 + par)[:, 0:512]
        Sx.op("act", lambda e: e.activation(out=rs, in_=p2.ap, func=AF.Ln, scale=1.0 / 64.0, bias=EPS),
              reads=[p2.buf], writes=[self.fpb[1 + par]])
        self.ps_free(p2)
        Sx.op("act", lambda e: e.activation(out=rs, in_=rs, func=AF.Exp, scale=-0.5, bias=lnbias),
              reads=[self.fpb[1 + par]], writes=[self.fpb[1 + par]])
        Sx.op("dve", lambda e: e.scalar_tensor_tensor(out=dst_ap, in0=ps.ap, scalar=self.vcol(gain_col), in1=rs,
                                                      op0=ALU.mult, op1=ALU.mult),
              reads=[ps.buf, self.fpb[1 + par], self.cbuf], writes=dst_bufs)
        self.ps_free(ps)

    def headnorm_proj(self, gain_col, lnbias, dst_fn):
        pend = None
        for tix in range(2):
            wv, wb = self.w_get()
            pre = None
            if tix == 0:
                pre = [self.ps_alloc() for _ in range(4)]
                self.proj_fm_multi(wv, wb, [cc * 128 for cc in range(4)], pre)
            for cc in range(4):
                qc = tix * 4 + cc
                if pre is not None:
                    ps = pre[cc]
                else:
                    ps = self.ps_alloc()
                    self.proj_fm(wv, wb, cc * 128, ps)
                self.headnorm_a(ps, qc % 2)
                if pend is not None:
                    self.headnorm_b(*pend)
                dst_ap, dst_bufs = dst_fn(qc)
                pend = (ps, qc % 2, gain_col, dst_ap, dst_bufs, lnbias)
            self.w_done()
        self.headnorm_b(*pend)

    def kv_proj(self, seq, blk):
        Sx = self.S_
        b = seq
        t0 = blk * T
        self.norm(2, b)
        self.headnorm_proj(VC_KG, 0.0,
                           lambda kc: (self.KT[:, kc * self.S + t0:kc * self.S + t0 + T], [self.KTb[kc]]))
        for vt in range(2):
            wv, wb = self.w_get()
            for i in range(4):
                ps = self.ps_alloc()
                self.proj_tm(wv, wb, i, ps)
                ti = blk * 4 + i
                if i % 2 == 0:
                    Sx.op("act", lambda e, ti=ti, vt=vt, ps=ps: e.activation(
                        out=self.V[:, ti * D + vt * 512:ti * D + (vt + 1) * 512], in_=ps.ap, func=AF.Copy),
                        reads=[ps.buf], writes=[self.Vb[ti]])
                else:
                    Sx.op("dve", lambda e, ti=ti, vt=vt, ps=ps: e.tensor_copy(
                        self.V[:, ti * D + vt * 512:ti * D + (vt + 1) * 512], ps.ap),
                        reads=[ps.buf], writes=[self.Vb[ti]])
                self.ps_free(ps)
            self.w_done()

    def attention(self, seq, blk):
        Sx = self.S_
        b = seq
        q0 = blk * T
        self.norm(3, b, reuse_rstd=True)
        BQ, BO, BSP, BA, BNC = 2, 10, 18, 24, 28
        FE = 3
        self.headnorm_proj(VC_QG, math.log(0.125), lambda qc: (self.bp(BQ + qc), [self.bpb[BQ + qc]]))
        for i in range(2):
            Sx.op("pool", lambda e, i=i: e.memset(self.bp(BNC + i), 0.0), writes=[self.bpb[BNC + i]])
        nkb = (q0 + T) // 128
        groups = []
        a = nkb - 1
        while a >= 0:
            groups.append([a, a - 1] if a - 1 >= 0 else [a])
            a -= 2
        ng = len(groups)
        assert sorted(self.free_banks) == list(range(8)), self.free_banks
        items = [(hd, gi) for hd in range(SB_H) for gi in range(ng)]
        n_it = len(items)
        O = self.ps_take(6)
        CS = self.ps_take(7)
        info = {}

        def zero(bank):
            Sx.op("pe", lambda e: e.matmul(bank.ap, self.cb(CB_ZERO), self.hnc(0), start=True, stop=False),
                  reads=[self.cbuf, self.hnb[0]], writes=[bank.buf], signal=False)

        def s1(i):
            hd, gi = items[i]
            c = hd // 2
            po = (hd % 2) * 64
            grp = groups[gi]
            zp = (i % 3) * 2
            ep = FE + (i % 2) * 2
            sp0 = BSP + (i % 3) * 2
            blks = []
            for s, a in enumerate(grp):
                c0 = max(0, 128 * a - q0)
                Z = self.ps_take(zp + s)
                blks.append((a, Z, c0, s))
                diag = (128 * a >= q0)
                Sx.op("pe", lambda e, a=a, Z=Z, c0=c0: e.matmul(
                    Z.ap[:, c0:512], self.KT[po:po + 64, c * self.S + a * 128:c * self.S + (a + 1) * 128],
                    self.bp(BQ + c)[po:po + 64, c0:512], start=True, stop=False),
                    reads=[self.KTb[c], self.bpb[BQ + c]], writes=[Z.buf], signal=not diag)
                if diag:
                    Sx.op("pe", lambda e, Z=Z, c0=c0: e.matmul(Z.ap[:, c0:c0 + 128], self.cb(CB_ID), self.cb(CB_DMASK),
                                                               start=False, stop=False),
                          reads=[self.cbuf], writes=[Z.buf])
            merged = (len(grp) == 2)
            c0m = min(bk[2] for bk in blks)
            info[i] = (blks, merged, zp, ep, sp0)
            if merged:
                zz = self.pst[:, zp * 512:(zp + 2) * 512].rearrange("p (a b) -> p a b", b=512)[:, :, c0m:512]
                ee = self.fpool[:, ep * 520:ep * 520 + 1024].rearrange("p (a b) -> p a b", b=512)[:, :, c0m:512]
                ss = self.bp(sp0, 2).rearrange("p (a b) -> p a b", b=512)[:, :, c0m:512]
                Sx.op("act", lambda e: e.activation(out=ee, in_=zz, func=AF.Exp),
                      reads=[blks[0][1].buf, blks[1][1].buf], writes=[self.fpb[ep], self.fpb[ep + 1]])
                Sx.op("act", lambda e: e.activation(out=ss, in_=ee, func=AF.Ln, bias=1.0),
                      reads=[self.fpb[ep], self.fpb[ep + 1]], writes=[self.bpb[sp0], self.bpb[sp0 + 1]])
            else:
                for (a, Z, c0, s) in blks:
                    Sx.op("act", lambda e, Z=Z, c0=c0, s=s: e.activation(
                        out=self.fp(ep + s)[:, c0:512], in_=Z.ap[:, c0:512], func=AF.Exp),
                        reads=[Z.buf], writes=[self.fpb[ep + s]])
                    Sx.op("act", lambda e, c0=c0, s=s: e.activation(
                        out=self.bp(sp0 + s)[:, c0:512], in_=self.fp(ep + s)[:, c0:512], func=AF.Ln, bias=1.0),
                        reads=[self.fpb[ep + s]], writes=[self.bpb[sp0 + s]])

        def s2(i):
            hd, gi = items[i]
            c = hd // 2
            blks, merged, zp, ep, sp0 = info[i]
            a0 = BA + (i % 2) * 2
            if gi == 0:
                zero(CS)
            for bi, (a, Z, c0, s) in enumerate(blks):
                has_ones = any(a2 > a for (a2, _z, _c, _s) in blks)
                has_carry = gi > 0
                Sx.op("pe", lambda e, Z=Z, c0=c0, s=s, st=not (has_ones or has_carry): e.matmul(
                    Z.ap[:, c0:512], self.cb(CB_NEGU), self.bp(sp0 + s)[:, c0:512], start=False, stop=st),
                    reads=[self.cbuf, self.bpb[sp0 + s]], writes=[Z.buf], signal=False)
                for (a2, Z2, c02, s2_) in blks:
                    if a2 > a:
                        Sx.op("pe", lambda e, Z=Z, c02=c02, s2_=s2_, st=not has_carry: e.matmul(
                            Z.ap[:, c02:512], self.cb(CB_NEGONES), self.bp(sp0 + s2_)[:, c02:512],
                            start=False, stop=st),
                            reads=[self.cbuf, self.bpb[sp0 + s2_]], writes=[Z.buf], signal=False)
                if has_carry:
                    nci = BNC + (gi - 1) % 2
                    Sx.op("pe", lambda e, Z=Z, nci=nci, c0=c0: e.matmul(
                        Z.ap[:, c0:512], self.cb(CB_SEL), self.bp(nci)[:, c0:512], start=False, stop=True),
                        reads=[self.cbuf, self.bpb[nci]], writes=[Z.buf], signal=False)
                cs_last = (gi == ng - 1) and (bi == len(blks) - 1)
                Sx.op("pe", lambda e, c0=c0, s=s, cs_last=cs_last: e.matmul(
                    CS.ap[:, c0:512], self.cb(CB_ONES), self.bp(sp0 + s)[:, c0:512], start=False, stop=cs_last),
                    reads=[self.cbuf, self.bpb[sp0 + s]], writes=[CS.buf, Z.buf])
            for _ in range(NFILL):
                Sx.op("pe", lambda e: e.matmul(CS.ap, self.cb(CB_ZERO), self.bp(BQ + c), start=False, stop=False),
                      reads=[self.cbuf, self.bpb[BQ + c]], writes=[CS.buf], signal=False)
            if merged:
                c0m = min(bk[2] for bk in blks)
                zz = self.pst[:, zp * 512:(zp + 2) * 512].rearrange("p (a b) -> p a b", b=512)[:, :, c0m:512]
                aa = self.bp(a0, 2).rearrange("p (a b) -> p a b", b=512)[:, :, c0m:512]
                Sx.op("act", lambda e: e.activation(out=aa, in_=zz, func=AF.Exp),
                      reads=[blks[0][1].buf, blks[1][1].buf], writes=[self.bpb[a0], self.bpb[a0 + 1]])
            else:
                for (a, Z, c0, s) in blks:
                    Sx.op("act", lambda e, Z=Z, c0=c0, s=s: e.activation(
                        out=self.bp(a0 + s)[:, c0:512], in_=Z.ap[:, c0:512], func=AF.Exp),
                        reads=[Z.buf], writes=[self.bpb[a0 + s]])
            for (a, Z, c0, s) in blks:
                self.ps_free(Z)
            if gi + 1 < ng:
                nci = BNC + gi % 2
                Sx.op("dve", lambda e, nci=nci: e.tensor_scalar_mul(self.bp(nci)[0:33, :], CS.ap[0:33, :], -1.0),
                      reads=[CS.buf], writes=[self.bpb[nci]])
                Sx.op("dve", lambda e, nci=nci: e.scalar_tensor_tensor(
                    out=self.bp(nci)[32:33, :], in0=CS.ap[32:33, :], scalar=-1.0, in1=self.bp(nci)[32:33, :],
                    op0=ALU.mult, op1=ALU.subtract),
                    reads=[CS.buf, self.bpb[nci]], writes=[self.bpb[nci]])

        def s3(i):
            hd, gi = items[i]
            c = hd // 2
            po = (hd % 2) * 64
            blks, merged, zp, ep, sp0 = info[i]
            a0 = BA + (i % 2) * 2
            if gi == 0:
                zero(O)
            for bi, (a, Z, c0, s) in enumerate(blks):
                o_last = (gi == ng - 1) and (bi == len(blks) - 1)
                Sx.op("pe", lambda e, a=a, c0=c0, s=s, o_last=o_last: e.matmul(
                    O.ap[:, c0:512], self.V[:, a * D + c * 128:a * D + (c + 1) * 128], self.bp(a0 + s)[:, c0:512],
                    start=False, stop=o_last),
                    reads=[self.Vb[a], self.bpb[a0 + s]], writes=[O.buf])
            if gi == ng - 1:
                Sx.op("dve", lambda e: e.tensor_copy(self.bp(BO + c)[po:po + 64, :], O.ap[po:po + 64, :]),
                      reads=[O.buf], writes=[self.bpb[BO + c]])

        s1(0)
        if n_it > 1:
            s1(1)
        s2(0)
        for i in range(n_it):
            if i + 2 < n_it:
                s1(i + 2)
            if i + 1 < n_it:
                s2(i + 1)
            s3(i)
        self.ps_free(O)
        self.ps_free(CS)
        for tix in range(2):
            wv, wb = self.w_get()
            for cc in range(4):
                ec = tix * 4 + cc
                ps = self.ps_alloc()
                for c in range(8):
                    Sx.op("pe", lambda e, c=c, cc=cc, ps=ps: e.matmul(ps.ap, wv[:, c, cc * 128:(cc + 1) * 128], self.bp(BO + c),
                                                                     start=(c == 0), stop=(c == 7)),
                          reads=[wb, self.bpb[BO + c]], writes=[ps.buf], signal=(c == 7))
                Sx.op("dve", lambda e, ec=ec, ps=ps: e.scalar_tensor_tensor(
                    out=self.hc(ec), in0=ps.ap, scalar=self.mcol(48 + 16 + ec, b), in1=self.hc(ec),
                    op0=ALU.mult, op1=ALU.add), reads=[ps.buf, self.hb[ec], self.cbuf], writes=[self.hb[ec]])
                self.ps_free(ps)
            self.w_done()

    def _skip_w(self, n):
        for _ in range(n):
            self.w_get()
            self.w_done()

    def _block(self, seq, blk):
        st = self.stage
        self.load_x(seq, blk)
        self.retention(seq, blk)
        if st == "l0mix":
            self._skip_w(self.NWT - 16)
            self.store_out(seq, blk)
            return
        self.ffn(0, seq, blk)
        if st == "l0":
            self._skip_w(self.NWT - 35)
            self.store_out(seq, blk)
            return
        self.kv_proj(seq, blk)
        self.attention(seq, blk)
        if st == "l1mix":
            self._skip_w(19)
            self.store_out(seq, blk)
            return
        self.ffn(1, seq, blk)
        self.store_out(seq, blk)

    def _finish(self):
        Sx = self.S_
        for fi in range(3, 7):
            Sx.wait_tok("sp", (f"f{fi}", Sx.cnt[f"f{fi}"]))


_CONSTS = None


def kernel(**inputs):
    global _CONSTS
    if _CONSTS is None:
        _CONSTS = host_consts()
    x = np.ascontiguousarray(np.asarray(inputs["x"], dtype=np.float32))
    B, S, _ = x.shape
    nseq = B // NCORES
    bld = Builder(nseq=nseq, S=S, stage="full")
    nc = bld.build()
    in_maps = []
    for core in range(NCORES):
        sl = slice(core * nseq, (core + 1) * nseq)
        m = {"x": np.ascontiguousarray(x[sl]),
             "c": np.ascontiguousarray(np.asarray(inputs["c"], np.float32)[sl]),
             "positions": np.ascontiguousarray(np.asarray(inputs["positions"]).astype(np.int32)[sl])}
        for k in W_SHAPES:
            m[k] = np.ascontiguousarray(np.asarray(inputs[k], dtype=np.float32))
        m.update(_CONSTS)
        in_maps.append(m)
    res = run_bass_kernel_spmd(nc, in_maps, core_ids=list(range(NCORES)))
    out = np.concatenate([np.asarray(r["out"]) for r in res.results], axis=0)
    return out.astype(np.float32)
```

```python
import contextlib
import math
import numpy as np
import concourse.bass as bass
import concourse.mybir as mybir
from concourse.bass_utils import run_bass_kernel_spmd

F32 = mybir.dt.float32
BF16 = mybir.dt.bfloat16
I32 = mybir.dt.int32
AF = mybir.ActivationFunctionType
ALU = mybir.AluOpType

NCORES = 8
D = 1024
T = 512
RET_H = 4
SB_H = 16
FF = 2816
NFC = 22
EPS = 1e-6
GAMMAS = [1.0 - 2.0 ** (-5.0 - h) for h in range(RET_H)]
NSLOT = 4
NF = 11
NBF = 32
NCVT = 8
MAGIC = 12582912.0
TWO_PI = 2.0 * math.pi
NFILL = 0
SAME_ENG_GAP = None

CB_ID, CB_ONES, CB_BD, CB_SEL, CB_NEGU, CB_NEGONES, CB_ZERO, CB_DMASK = [i * 128 for i in range(8)]

VC_ADAB = 0
VC_NMG = 112
VC_NFG = 128
VC_KVG = 144
VC_CONVB = 152
VC_KG = 240
VC_QG = 241
VC_CONVW = 256


class Buf:
    __slots__ = ("w", "r")

    def __init__(self):
        self.w = None
        self.r = []


class Sched:
    ENG = ("pe", "act", "dve", "pool", "sp")

    def __init__(self, nc, es):
        self.nc = nc
        self.es = es
        self.eng = {"pe": nc.tensor, "act": nc.scalar, "dve": nc.vector,
                    "pool": nc.gpsimd, "sp": nc.sync}
        self.sems = {}
        self.cnt = {}
        for e in self.ENG:
            self.sems[e] = es.enter_context(nc.semaphore("sem_" + e))
            self.cnt[e] = 0
        self.seen = {e: {} for e in self.ENG}
        self.pending = {e: False for e in self.ENG}
        self.n_inst = {e: 0 for e in self.ENG}
        self.n_wait = {e: 0 for e in self.ENG}

    def new_sem(self, key):
        self.sems[key] = self.es.enter_context(self.nc.semaphore("sem_" + key))
        self.cnt[key] = 0
        return key

    def _need(self, e, tok, waits):
        if tok is None:
            return
        k, v = tok
        if k == e and e == "pe":
            return
        if k == e and SAME_ENG_GAP is not None and (self.cnt[e] - v) >= SAME_ENG_GAP:
            return
        if self.seen[e].get(k, 0) >= v:
            return
        if waits.get(k, 0) < v:
            waits[k] = v

    def _emit_waits(self, e, waits):
        for k, v in waits.items():
            self.eng[e].wait_ge(self.sems[k], v)
            self.seen[e][k] = v
            self.n_wait[e] += 1

    def _collect(self, e, reads, writes):
        waits = {}
        for b in reads:
            self._need(e, b.w, waits)
        for b in writes:
            self._need(e, b.w, waits)
            for t in b.r:
                self._need(e, t, waits)
        return waits

    def _record(self, tok, reads, writes):
        for b in writes:
            b.w = tok
            b.r = []
        for b in reads:
            if b in writes:
                continue
            b.r.append(tok)
            if len(b.r) > 16:
                best = {}
                for k, v in b.r:
                    if best.get(k, 0) < v:
                        best[k] = v
                b.r = list(best.items())

    def op(self, e, fn, reads=(), writes=(), signal=True):
        waits = self._collect(e, reads, writes)
        if e in waits and waits[e] > self.cnt[e]:
            raise RuntimeError("self-wait on unsignalled op")
        self._emit_waits(e, waits)
        ins = fn(self.eng[e])
        self.n_inst[e] += 1
        if signal:
            self.cnt[e] += 1
            ins.then_inc(self.sems[e], 1)
            self.pending[e] = False
            tok = (e, self.cnt[e])
        else:
            self.pending[e] = True
            tok = (e, self.cnt[e] + 1)
        self._record(tok, reads, writes)
        return tok

    def dma(self, q, out, in_, semkey, reads=(), writes=(), defer=False, **kw):
        waits = self._collect(q, reads, writes)
        self._emit_waits(q, waits)
        ins = self.eng[q].dma_start(out=out, in_=in_, **kw)
        self.cnt[semkey] += 16
        ins.then_inc(self.sems[semkey], 16)
        tok = (semkey, self.cnt[semkey])
        if not defer:
            self._record(tok, reads, writes)
        return tok

    def wait_tok(self, e, tok):
        waits = {}
        self._need(e, tok, waits)
        self._emit_waits(e, waits)


class PsumBank:
    def __init__(self, t, idx):
        self.t = t
        self.idx = idx
        self.buf = Buf()
        self.ap = t[:, idx * 512:(idx + 1) * 512]
        self.bf = self.ap.bitcast(BF16)


def host_consts():
    c = {}
    c["c_identf"] = np.eye(128, dtype=np.float32)
    cb = np.zeros((128, 8 * 128), np.float32)
    j = np.arange(128)[:, None]
    s = np.arange(128)[None, :]
    cb[:, CB_ID:CB_ID + 128] = np.eye(128)
    cb[:, CB_ONES:CB_ONES + 128] = 1.0
    cb[:, CB_BD:CB_BD + 128] = ((j // 64) == (s // 64)).astype(np.float32)
    sel = np.zeros((128, 128), np.float32)
    sel[0, :] = 1.0
    sel[32, :] = 1.0
    cb[:, CB_SEL:CB_SEL + 128] = sel
    cb[:, CB_NEGU:CB_NEGU + 128] = -(j >= s).astype(np.float32)
    cb[:, CB_NEGONES:CB_NEGONES + 128] = -1.0
    cb[:, CB_DMASK:CB_DMASK + 128] = np.where(s <= j, -30000.0, 0.0)
    c["c_b16"] = cb
    m = np.arange(128, dtype=np.float64)[:, None]
    n = np.arange(512, dtype=np.float64)[None, :]
    rm = np.zeros((128, RET_H * 512), np.float64)
    qd = np.zeros((128, RET_H * 512), np.float64)
    kd = np.zeros((128, 16), np.float64)
    for h, g in enumerate(GAMMAS):
        lg = math.log(g)
        rm[:, h * 512:(h + 1) * 512] = np.where(n >= m, np.exp(np.maximum(n - m, 0.0) * lg), 0.0) / 16.0
        qd[:, h * 512:(h + 1) * 512] = np.exp((n + 1.0) * lg)
        for jj in range(4):
            kd[:, h * 4 + jj] = np.exp((511.0 - (128.0 * jj + m[:, 0])) * lg) / 16.0
    c["c_retmask"] = rm.astype(np.float32)
    c["c_qdec"] = qd.astype(np.float32)
    c["c_kdec"] = kd.astype(np.float32)
    inv = 10000.0 ** (-np.arange(128, dtype=np.float32) / np.float32(128.0))
    c["c_invf"] = np.stack([inv.astype(np.float32), np.zeros(128, np.float32)], axis=1).astype(np.float32)
    return c


W_SHAPES = {
    "ada_w": [2, D, 6 * D], "ada_b": [2, 6 * D], "norm_mix_g": [2, D], "norm_ffn_g": [2, D],
    "ret_w_in": [1, D, 6 * D], "ret_w_out": [1, 2 * D, D], "kv_ada_w": [D, 2 * D], "kv_ada_b": [2 * D],
    "kv_norm_g": [D], "w_kv": [D, 2 * D], "k_norm_g": [64], "sb_w_q": [1, D, D], "q_norm_g": [1, 64],
    "sb_w_out": [1, D, D], "ffn_w_in": [2, D, 2 * FF], "ffn_conv_w": [2, 3, 2 * FF],
    "ffn_conv_b": [2, 2 * FF], "ffn_w_out": [2, FF, D],
}
C_SHAPES = {"c_identf": [128, 128], "c_b16": [128, 1024], "c_retmask": [128, 2048],
            "c_qdec": [128, 2048], "c_kdec": [128, 16], "c_invf": [128, 2]}


class Builder:
    def __init__(self, nseq=2, S=2048, stage="full"):
        self.nseq = nseq
        self.S = S
        self.NB = S // T
        self.stage = stage
        self.nc = bass.Bass("TRN2", target_bir_lowering=False)
        self.es = contextlib.ExitStack()

    def build(self):
        nc = self.nc
        with self.es:
            self._declare()
            self._prologue()
            for seq in range(self.nseq):
                self._seq_init(seq)
                for blk in range(self.NB):
                    self._block(seq, blk)
            self._finish()
        return nc

    def sb(self, name, shape, dt):
        return self.es.enter_context(self.nc.sbuf_tensor(name, shape, dt))

    def _declare(self):
        nc, es = self.nc, self.es
        nseq, S = self.nseq, self.S
        dr = {}
        dr["x"] = nc.dram_tensor("x", [nseq, S, D], F32, kind="ExternalInput").ap()
        dr["c"] = nc.dram_tensor("c", [nseq, D], F32, kind="ExternalInput").ap()
        dr["positions"] = nc.dram_tensor("positions", [nseq, S], I32, kind="ExternalInput").ap()
        for k, shp in W_SHAPES.items():
            dr[k] = nc.dram_tensor(k, shp, F32, kind="ExternalInput").ap()
        for k, shp in C_SHAPES.items():
            dr[k] = nc.dram_tensor(k, shp, F32, kind="ExternalInput").ap()
        dr["out"] = nc.dram_tensor("out", [nseq, S, D], F32, kind="ExternalOutput").ap()
        self.dr = dr
        self.S_ = Sched(nc, es)
        Sx = self.S_
        self.wtiles = self._wtile_list()
        self.NWT = len(self.wtiles)
        self.wscr = nc.dram_tensor("wscr", [self.NWT, 128, 4096], BF16).ap()
        self.scr_buf = [Buf() for _ in range(self.NWT)]
        self.identF = self.sb("identF", [128, 128], F32)
        self.cb16 = self.sb("cb16", [128, 1024], BF16)
        self.retmask = self.sb("retmask", [128, 2048], BF16)
        self.qdec = self.sb("qdec", [128, 2048], BF16)
        self.kdec = self.sb("kdec", [128, 16], F32)
        self.invf = self.sb("invf", [128, 2], F32)
        self.invf2 = self.sb("invf2", [128, 2], F32)
        self.vecT = self.sb("vecT", [128, 640], F32)
        self.mods = self.sb("mods", [128, 224], F32)
        self.gsc = self.sb("gsc", [128, 80], F32)
        self.qg8 = self.sb("qg8", [128, 2], F32)
        self.cact = self.sb("cact", [128, 16], BF16)
        self.h = self.sb("h", [128, 8 * T], F32)
        self.hn = self.sb("hn", [128, 8 * T], BF16)
        self.KT = self.sb("KT", [128, 8 * S], BF16)
        self.V = self.sb("V", [128, (S // 128) * D], BF16)
        self.state = self.sb("state", [128, 8 * 512], F32)
        self.halo = self.sb("halo", [128, 2 * 44 * 2], F32)
        self.gn = self.sb("gn", [128, 8], F32)
        self.wring = self.sb("wring", [128, NSLOT * 4096], BF16)
        self.fpool = self.sb("fpool", [128, NF * 520], F32)
        self.c2 = self.fpool[:, 8 * 520:8 * 520 + D]
        self.bpool = self.sb("bpool", [128, NBF * 512], BF16)
        self.cbuf = Buf()
        self.hb = [Buf() for _ in range(8)]
        self.hnb = [Buf() for _ in range(8)]
        self.KTb = [Buf() for _ in range(8)]
        self.Vb = [Buf() for _ in range(S // 128)]
        self.stb = [Buf() for _ in range(8)]
        self.halob = Buf()
        self.gnb = Buf()
        self.wslot_buf = [Buf() for _ in range(NSLOT)]
        self.fpb = [Buf() for _ in range(NF)]
        self.ubh = [Buf() for _ in range(NF)]
        self.bpb = [Buf() for _ in range(NBF)]
        for i in range(NSLOT):
            Sx.new_sem(f"w{i}")
            Sx.new_sem(f"ws{i}")
        for i in range(NF):
            Sx.new_sem(f"f{i}")
        for i in range(7, 15):
            Sx.new_sem(f"x{i}")
        for i in range(NCVT):
            Sx.new_sem(f"cv{i}")
        Sx.new_sem("cst")
        Sx.new_sem("cstp")
        self.cvt_last = [None] * NCVT
        self.wb_last = [None] * 4
        for i in range(4):
            Sx.new_sem(f"wb{i}")
        self.banks = []
        self.pst = es.enter_context(nc.psum_tensor("pst", [128, 4096], F32))
        for i in range(8):
            self.banks.append(PsumBank(self.pst, i))
        self.free_banks = list(range(8))
        self.wnext = 0
        self.wissued = 0
        self.wtotal = None

    def fp(self, i):
        return self.fpool[:, i * 520:(i + 1) * 520]

    def bp(self, i, n=1):
        return self.bpool[:, i * 512:(i + n) * 512]

    def hc(self, c):
        return self.h[:, c * T:(c + 1) * T]

    def hnc(self, c):
        return self.hn[:, c * T:(c + 1) * T]

    def cb(self, off):
        return self.cb16[:, off:off + 128]

    def vcol(self, col):
        return self.vecT[:, col:col + 1]

    def mcol(self, chunk, b):
        return self.mods[:, chunk * 2 + b:chunk * 2 + b + 1]

    def ps_alloc(self):
        assert self.free_banks, "out of PSUM banks"
        i = self.free_banks.pop(0)
        return self.banks[i]

    def ps_take(self, i):
        assert i in self.free_banks, f"bank {i} not free"
        self.free_banks.remove(i)
        return self.banks[i]

    def ps_free(self, bank):
        assert bank.idx not in self.free_banks
        self.free_banks.append(bank.idx)

    def _wtile_list(self):
        dr = self.dr

        def kp(ap2d):
            return ap2d.rearrange("(k p) e -> p k e", p=128)

        tl = []
        rin = kp(dr["ret_w_in"][0])
        rout = kp(dr["ret_w_out"][0])
        for h in range(RET_H):
            tl.append(dict(nrc=8, ncols=512, srcs=[(rin[:, :, h * 256:(h + 1) * 256], 0, 256),
                                                    (rin[:, :, D + h * 256:D + (h + 1) * 256], 256, 256)]))
            tl.append(dict(nrc=8, ncols=512, srcs=[(rin[:, :, 2 * D + h * 512:2 * D + (h + 1) * 512], 0, 512)]))
            tl.append(dict(nrc=8, ncols=512, srcs=[(rin[:, :, 4 * D + h * 512:4 * D + (h + 1) * 512], 0, 512)]))
            if h > 0:
                tl.append(dict(nrc=4, ncols=1024, srcs=[(rout[:, (h - 1) * 4:h * 4, :], 0, 1024)]))
        tl.append(dict(nrc=4, ncols=1024, srcs=[(rout[:, (RET_H - 1) * 4:RET_H * 4, :], 0, 1024)]))

        def ffn(l):
            win = kp(dr["ffn_w_in"][l])
            wout = kp(dr["ffn_w_out"][l])
            for j in range(11):
                tl.append(dict(nrc=8, ncols=512, srcs=[(win[:, :, j * 256:(j + 1) * 256], 0, 256),
                                                        (win[:, :, FF + j * 256:FF + (j + 1) * 256], 256, 256)]))
            for e in range(8):
                tl.append(dict(nrc=22, ncols=128, srcs=[(wout[:, :, e * 128:(e + 1) * 128], 0, 128)]))

        ffn(0)
        wkv = kp(dr["w_kv"])
        for j in range(4):
            tl.append(dict(nrc=8, ncols=512, srcs=[(wkv[:, :, j * 512:(j + 1) * 512], 0, 512)]))
        wq = kp(dr["sb_w_q"][0])
        for j in range(2):
            tl.append(dict(nrc=8, ncols=512, srcs=[(wq[:, :, j * 512:(j + 1) * 512], 0, 512)]))
        wo = kp(dr["sb_w_out"][0])
        for j in range(2):
            tl.append(dict(nrc=8, ncols=512, srcs=[(wo[:, :, j * 512:(j + 1) * 512], 0, 512)]))
        ffn(1)
        return tl

    def _convert_weights(self):
        Sx = self.S_
        for t, d in enumerate(self.wtiles):
            key = f"cv{t % NCVT}"
            if self.cvt_last[t % NCVT] is not None:
                Sx.wait_tok("pool", self.cvt_last[t % NCVT])
            n = d["nrc"] * d["ncols"]
            dst = self.wscr[t][:, 0:n].rearrange("p (k e) -> p k e", e=d["ncols"])
            tok = None
            for (src, off, w) in d["srcs"]:
                tok = Sx.dma("pool", dst[:, :, off:off + w], src, key, defer=True)
            self.cvt_last[t % NCVT] = tok
            self.scr_buf[t].w = tok

    def _w_issue(self):
        if self.wissued >= self.wtotal:
            return
        Sx = self.S_
        idx = self.wissued
        t = idx % self.NWT
        slot = idx % NSLOT
        d = self.wtiles[t]
        n = d["nrc"] * d["ncols"]
        ring = self.wring[:, slot * 4096:slot * 4096 + n]
        if idx < self.NWT:
            rv = ring.rearrange("p (k e) -> p k e", e=d["ncols"])
            for (src, off, w) in d["srcs"]:
                Sx.dma("pool", rv[:, :, off:off + w], src, f"ws{slot}", writes=[self.wslot_buf[slot]])
            key = f"wb{t % 4}"
            if self.wb_last[t % 4] is not None:
                Sx.wait_tok("sp", self.wb_last[t % 4])
            tok = Sx.dma("sp", self.wscr[t][:, 0:n], ring, key, reads=[self.wslot_buf[slot]],
                         writes=[self.scr_buf[t]])
            self.wb_last[t % 4] = tok
        else:
            Sx.dma("sp", ring, self.wscr[t][:, 0:n], f"w{slot}",
                   reads=[self.scr_buf[t]], writes=[self.wslot_buf[slot]])
        self.wissued += 1

    def w_get(self):
        idx = self.wnext
        assert idx < self.wissued
        slot = idx % NSLOT
        d = self.wtiles[idx % self.NWT]
        n = d["nrc"] * d["ncols"]
        v = self.wring[:, slot * 4096:slot * 4096 + n].rearrange("p (k e) -> p k e", e=d["ncols"])
        return v, self.wslot_buf[slot]

    def w_done(self):
        self.wnext += 1
        self._w_issue()

    def _prologue(self):
        Sx, dr = self.S_, self.dr
        cst_bufs = []

        def cload(q, out, in_, **kw):
            Sx.dma(q, out, in_, "cst" if q == "sp" else "cstp", defer=True, **kw)

        cload("sp", self.identF[:], dr["c_identf"])
        cload("pool", self.retmask[:], dr["c_retmask"])
        cload("sp", self.kdec[:], dr["c_kdec"])
        cload("sp", self.invf[:], dr["c_invf"])
        cload("sp", self.c2[0:self.nseq, :], dr["c"])
        cload("pool", self.cb16[:], dr["c_b16"])
        cload("pool", self.qdec[:], dr["c_qdec"])
        rows = []
        rows.append((dr["ada_b"].rearrange("l (c p) -> (l c) p", p=128), 96))
        rows.append((dr["kv_ada_b"].rearrange("(c p) -> c p", p=128), 16))
        rows.append((dr["norm_mix_g"].rearrange("l (c p) -> (l c) p", p=128), 16))
        rows.append((dr["norm_ffn_g"].rearrange("l (c p) -> (l c) p", p=128), 16))
        rows.append((dr["kv_norm_g"].rearrange("(c p) -> c p", p=128), 8))
        rows.append((dr["ffn_conv_b"].rearrange("l (c p) -> (l c) p", p=128), 88))
        rows.append(("kg", 1))
        rows.append(("qg", 1))
        rows.append(("pad", 14))
        rows.append((dr["ffn_conv_w"].rearrange("l t (c p) -> (l t c) p", p=128), 264))
        stage_tiles = [3, 4, 5, 6, 7]
        for g in stage_tiles:
            Sx.op("dve", lambda e, g=g: e.memset(self.fp(g)[:, 0:128], 0.0), writes=[self.fpb[g]])
        Sx.wait_tok("sp", ("dve", Sx.cnt["dve"]))
        r = 0
        for (src, n) in rows:
            if isinstance(src, str):
                if src == "kg":
                    for half in range(2):
                        cload("sp", self.fp(3 + r // 128)[r % 128:r % 128 + 1, half * 64:(half + 1) * 64],
                              dr["k_norm_g"].rearrange("(o d) -> o d", o=1))
                elif src == "qg":
                    for half in range(2):
                        cload("sp", self.fp(3 + r // 128)[r % 128:r % 128 + 1, half * 64:(half + 1) * 64],
                              dr["q_norm_g"])
                r += n
                continue
            done = 0
            while done < n:
                g = r // 128
                p0 = r % 128
                take = min(n - done, 128 - p0)
                cload("sp", self.fp(3 + g)[p0:p0 + take, 0:128], src[done:done + take, :])
                done += take
                r += take
        assert r == 520, r
        tok = ("cst", Sx.cnt["cst"])
        for e_ in ("pe", "act", "dve", "pool"):
            Sx.wait_tok(e_, ("cstp", Sx.cnt["cstp"]))
        self.cbuf.w = tok
        for g in stage_tiles + [8, 9]:
            self.fpb[g].w = tok
        pa = self.ps_alloc()
        pb = self.ps_alloc()
        for g in range(5):
            nr = 128 if g < 4 else 8
            bank = pa if g < 4 else pb
            col = (g % 4) * 128
            Sx.op("pe", lambda e, g=g, nr=nr, bank=bank, col=col: e.transpose(
                bank.ap[:, col:col + nr], self.fp(3 + g)[0:nr, 0:128], self.identF[0:nr, 0:nr]),
                reads=[self.fpb[3 + g], self.cbuf], writes=[bank.buf])
        Sx.op("dve", lambda e: e.tensor_copy(self.vecT[:, 0:512], pa.ap[:, 0:512]), reads=[pa.buf], writes=[self.cbuf])
        Sx.op("dve", lambda e: e.tensor_copy(self.vecT[:, 512:520], pb.ap[:, 0:8]), reads=[pb.buf], writes=[self.cbuf])
        self.ps_free(pa)
        self.ps_free(pb)
        Sx.op("dve", lambda e: e.tensor_scalar_mul(self.invf2[:], self.invf[:], 1.0 / TWO_PI),
              reads=[self.cbuf], writes=[self.cbuf])
        pc = self.ps_alloc()
        for k in range(8):
            Sx.op("pe", lambda e, k=k: e.transpose(pc.ap[:, k * 2:k * 2 + self.nseq],
                                                   self.c2[0:self.nseq, k * 128:(k + 1) * 128],
                                                   self.identF[0:self.nseq, 0:self.nseq]),
                  reads=[self.cbuf, self.fpb[8], self.fpb[9]], writes=[pc.buf])
        if self.nseq < 2:
            Sx.op("dve", lambda e: e.memset(self.cact[:], 0.0), writes=[self.cbuf])
        for k in range(8):
            Sx.op("act", lambda e, k=k: e.activation(out=self.cact[:, k * 2:k * 2 + self.nseq],
                                                     in_=pc.ap[:, k * 2:k * 2 + self.nseq], func=AF.Silu),
                  reads=[pc.buf], writes=[self.cbuf])
        self.ps_free(pc)
        ada_tiles = []
        for l in range(2):
            wl = dr["ada_w"][l].rearrange("(k p) e -> p k e", p=128)
            for j in range(12):
                ada_tiles.append((wl[:, :, j * 512:(j + 1) * 512], l * 48 + j * 4, VC_ADAB + l * 48 + j * 4))
        wk = dr["kv_ada_w"].rearrange("(k p) e -> p k e", p=128)
        for j in range(4):
            ada_tiles.append((wk[:, :, j * 512:(j + 1) * 512], 96 + j * 4, VC_ADAB + 96 + j * 4))
        nada = len(ada_tiles)

        def ada_issue(i):
            if i >= nada:
                return
            slot = i % NSLOT
            Sx.dma("pool", self.wring[:, slot * 4096:(slot + 1) * 4096].rearrange("p (k e) -> p k e", e=512),
                   ada_tiles[i][0], f"ws{slot}", writes=[self.wslot_buf[slot]])

        for i in range(NSLOT):
            ada_issue(i)
        for i, (_, ch0, vc0) in enumerate(ada_tiles):
            slot = i % NSLOT
            wv = self.wring[:, slot * 4096:(slot + 1) * 4096].rearrange("p (k e) -> p k e", e=512)
            pm = self.ps_alloc()
            for cc in range(4):
                for k in range(8):
                    Sx.op("pe", lambda e, cc=cc, k=k: e.matmul(pm.ap[:, cc * 2:cc * 2 + 2],
                                                               wv[:, k, cc * 128:(cc + 1) * 128],
                                                               self.cact[:, k * 2:k * 2 + 2],
                                                               start=(k == 0), stop=(k == 7)),
                          reads=[self.wslot_buf[slot], self.cbuf], writes=[pm.buf], signal=(k == 7))
            for cc in range(4):
                Sx.op("dve", lambda e, cc=cc: e.tensor_scalar_add(
                    self.mods[:, (ch0 + cc) * 2:(ch0 + cc) * 2 + 2], pm.ap[:, cc * 2:cc * 2 + 2],
                    self.vcol(vc0 + cc)), reads=[pm.buf, self.cbuf], writes=[self.cbuf])
            self.ps_free(pm)
            ada_issue(i + NSLOT)
        norm_defs = [(VC_NMG + 0, 8), (VC_NFG + 0, 32), (VC_KVG, 104), (VC_NMG + 8, 48 + 8), (VC_NFG + 8, 48 + 32)]
        self.norm_shift = [0, 24, 96, 48 + 0, 48 + 24]
        for n, (gcol, sc0) in enumerate(norm_defs):
            for c in range(8):
                gv = self.gsc[:, (n * 8 + c) * 2:(n * 8 + c) * 2 + 2]
                Sx.op("dve", lambda e, gv=gv, c=c, sc0=sc0: e.tensor_scalar_add(
                    gv, self.mods[:, (sc0 + c) * 2:(sc0 + c) * 2 + 2], 1.0), reads=[self.cbuf], writes=[self.cbuf])
                Sx.op("dve", lambda e, gv=gv, c=c, gcol=gcol: e.tensor_scalar_mul(gv, gv, self.vcol(gcol + c)),
                      reads=[self.cbuf], writes=[self.cbuf])
        self.wtotal = self.nseq * self.NB * self.NWT
        for _ in range(NSLOT):
            self._w_issue()

    def _seq_init(self, seq):
        Sx = self.S_
        for i in range(8):
            Sx.op("pool", lambda e, i=i: e.memset(self.state[:, i * 512:(i + 1) * 512], 0.0), writes=[self.stb[i]])
        Sx.op("pool", lambda e: e.memset(self.halo[:], 0.0), writes=[self.halob])

    def load_x(self, seq, blk):
        Sx, dr = self.S_, self.dr
        t0 = blk * T
        stg = []
        for i in range(4):
            for half in range(2):
                if i < 2:
                    fi = 7 + i * 2 + half
                    stg.append((self.fp(fi)[:, 0:512], [self.fpb[fi]], f"x{fi}"))
                else:
                    k = (i - 2) * 2 + half
                    bi = 24 + 2 * k
                    stg.append((self.bp(bi, 2).bitcast(F32), [self.bpb[bi], self.bpb[bi + 1]], f"x{11 + k}"))
        for i in range(4):
            for half in range(2):
                ap, bufs, key = stg[i * 2 + half]
                Sx.dma("pool", ap, dr["x"][seq, t0 + i * 128:t0 + (i + 1) * 128, half * 512:(half + 1) * 512],
                       key, writes=bufs)
        for i in range(4):
            for half in range(2):
                ap, bufs, key = stg[i * 2 + half]
                ps = self.ps_alloc()
                for cq in range(4):
                    Sx.op("pe", lambda e, cq=cq, ap=ap, ps=ps: e.transpose(
                        ps.ap[:, cq * 128:(cq + 1) * 128], ap[:, cq * 128:(cq + 1) * 128], self.identF[:]),
                        reads=bufs + [self.cbuf], writes=[ps.buf], signal=(cq == 3))
                hv = self.h[:, :].rearrange("p (c t) -> p c t", t=T)[:, half * 4:(half + 1) * 4, i * 128:(i + 1) * 128]
                pv = ps.ap.rearrange("p (c t) -> p c t", t=128)
                if half == 0:
                    Sx.op("act", lambda e, hv=hv, pv=pv: e.activation(out=hv, in_=pv, func=AF.Copy),
                          reads=[ps.buf], writes=self.hb[half * 4:(half + 1) * 4])
                else:
                    Sx.op("dve", lambda e, hv=hv, pv=pv: e.tensor_copy(hv, pv),
                          reads=[ps.buf], writes=self.hb[half * 4:(half + 1) * 4])
                self.ps_free(ps)

    def store_out(self, seq, blk):
        Sx, dr = self.S_, self.dr
        t0 = blk * T
        for i in range(4):
            for half in range(2):
                fi = 3 + (i % 2) * 2 + half
                ps = self.ps_alloc()
                for cq in range(4):
                    c = half * 4 + cq
                    Sx.op("pe", lambda e, cq=cq, c=c, ps=ps: e.transpose(
                        ps.ap[:, cq * 128:(cq + 1) * 128], self.hc(c)[:, i * 128:(i + 1) * 128], self.identF[:]),
                        reads=[self.hb[c], self.cbuf], writes=[ps.buf], signal=(cq == 3))
                if half == 0:
                    Sx.op("act", lambda e, fi=fi, ps=ps: e.activation(out=self.fp(fi)[:, 0:512], in_=ps.ap, func=AF.Copy),
                          reads=[ps.buf], writes=[self.fpb[fi]])
                else:
                    Sx.op("dve", lambda e, fi=fi, ps=ps: e.tensor_copy(self.fp(fi)[:, 0:512], ps.ap),
                          reads=[ps.buf], writes=[self.fpb[fi]])
                self.ps_free(ps)
                Sx.dma("sp", dr["out"][seq, t0 + i * 128:t0 + (i + 1) * 128, half * 512:(half + 1) * 512],
                       self.fp(fi)[:, 0:512], f"f{fi}", reads=[self.fpb[fi]])

    def norm(self, nidx, b, reuse_rstd=False):
        Sx = self.S_
        if reuse_rstd:
            self._norm_apply(nidx, b)
            return
        ps = self.ps_alloc()
        for c in range(8):
            si = c % 2
            if c % 3 == 2:
                Sx.op("dve", lambda e, c=c, si=si: e.tensor_tensor(self.bp(si), self.hc(c), self.hc(c), op=ALU.mult),
                      reads=[self.hb[c]], writes=[self.bpb[si]])
            else:
                Sx.op("act", lambda e, c=c, si=si: e.activation(out=self.bp(si), in_=self.hc(c), func=AF.Square),
                      reads=[self.hb[c]], writes=[self.bpb[si]])
            Sx.op("pe", lambda e, c=c, si=si: e.matmul(ps.ap, self.cb(CB_ONES), self.bp(si), start=(c == 0), stop=(c == 7)),
                  reads=[self.bpb[si], self.cbuf], writes=[ps.buf])
        rstd = self.fp(0)[:, 0:512]
        Sx.op("act", lambda e: e.activation(out=rstd, in_=ps.ap, func=AF.Ln, scale=1.0 / D, bias=EPS),
              reads=[ps.buf], writes=[self.fpb[0]])
        self.ps_free(ps)
        Sx.op("act", lambda e: e.activation(out=rstd, in_=rstd, func=AF.Exp, scale=-0.5),
              reads=[self.fpb[0]], writes=[self.fpb[0]])
        self._norm_apply(nidx, b)

    def _norm_apply(self, nidx, b):
        Sx = self.S_
        rstd = self.fp(0)[:, 0:512]
        sh0 = self.norm_shift[nidx]
        for c in range(8):
            ti = 1 + c % 2
            gcol = (nidx * 8 + c) * 2 + b
            Sx.op("dve", lambda e, c=c, ti=ti: e.tensor_tensor(self.fp(ti)[:, 0:512], self.hc(c), rstd, op=ALU.mult),
                  reads=[self.hb[c], self.fpb[0]], writes=[self.fpb[ti]])
            if c % 3 == 2:
                Sx.op("pool", lambda e, c=c, ti=ti, gcol=gcol: e.tensor_scalar(
                    self.hnc(c), self.fp(ti)[:, 0:512], self.gsc[:, gcol:gcol + 1], self.mcol(sh0 + c, b),
                    op0=ALU.mult, op1=ALU.add),
                    reads=[self.fpb[ti], self.cbuf], writes=[self.hnb[c]])
            else:
                Sx.op("act", lambda e, c=c, ti=ti, gcol=gcol: e.activation(
                    out=self.hnc(c), in_=self.fp(ti)[:, 0:512], func=AF.Identity,
                    scale=self.gsc[:, gcol:gcol + 1], bias=self.mcol(sh0 + c, b)),
                    reads=[self.fpb[ti], self.cbuf], writes=[self.hnb[c]])

    def proj_fm(self, wv, wb, col0, ps):
        Sx = self.S_
        for k in range(8):
            Sx.op("pe", lambda e, k=k: e.matmul(ps.ap, wv[:, k, col0:col0 + 128], self.hnc(k),
                                                start=(k == 0), stop=(k == 7)),
                  reads=[wb, self.hnb[k]], writes=[ps.buf], signal=(k == 7))

    def proj_fm_multi(self, wv, wb, cols, pss):
        Sx = self.S_
        for k in range(8):
            for col0, ps in zip(cols, pss):
                Sx.op("pe", lambda e, k=k, col0=col0, ps=ps: e.matmul(ps.ap, wv[:, k, col0:col0 + 128], self.hnc(k),
                                                                     start=(k == 0), stop=(k == 7)),
                      reads=[wb, self.hnb[k]], writes=[ps.buf], signal=(k == 7))

    def proj_tm(self, wv, wb, i, ps, col0=0):
        Sx = self.S_
        for k in range(8):
            Sx.op("pe", lambda e, k=k: e.matmul(ps.ap, self.hnc(k)[:, i * 128:(i + 1) * 128], wv[:, k, col0:col0 + 512],
                                                start=(k == 0), stop=(k == 7)),
                  reads=[wb, self.hnb[k]], writes=[ps.buf], signal=(k == 7))

    def rope_tables(self, seq, blk):
        Sx, dr = self.S_, self.dr
        t0 = blk * T
        posi = self.fp(5)[:, 0:512].bitcast(I32)
        Sx.dma("sp", posi, dr["positions"][seq:seq + 1, t0:t0 + T].partition_broadcast(128), "f5",
               writes=[self.fpb[5]])
        posf = self.fp(6)[:, 0:512]
        Sx.op("dve", lambda e: e.tensor_copy(posf, posi), reads=[self.fpb[5]], writes=[self.fpb[6]])
        ang = self.fp(5)[:, 0:512]
        Sx.op("dve", lambda e: e.tensor_scalar_mul(ang, posf, self.invf[:, 0:1]),
              reads=[self.fpb[6], self.cbuf], writes=[self.fpb[5]])
        for which, dst in (("sin", 4), ("cos", 3)):
            a = ang
            if which == "cos":
                a = self.fp(6)[:, 0:512]
                Sx.op("dve", lambda e, a=a: e.tensor_scalar_add(a, ang, math.pi / 2.0),
                      reads=[self.fpb[5]], writes=[self.fpb[6]])
                ab = self.fpb[6]
            else:
                ab = self.fpb[5]
            kk = self.fp(7)[:, 0:512]
            Sx.op("dve", lambda e, a=a: e.tensor_scalar(kk, a, 1.0 / TWO_PI, MAGIC, op0=ALU.mult, op1=ALU.add),
                  reads=[ab], writes=[self.fpb[7]])
            Sx.op("dve", lambda e: e.tensor_scalar_add(kk, kk, -MAGIC), reads=[self.fpb[7]], writes=[self.fpb[7]])
            r = self.fp(8)[:, 0:512]
            Sx.op("dve", lambda e, a=a: e.scalar_tensor_tensor(out=r, in0=kk, scalar=-TWO_PI, in1=a,
                                                               op0=ALU.mult, op1=ALU.add),
                  reads=[self.fpb[7], ab], writes=[self.fpb[8]])
            Sx.op("dve", lambda e: e.tensor_scalar(r, r, -math.pi, math.pi, op0=ALU.max, op1=ALU.min),
                  reads=[self.fpb[8]], writes=[self.fpb[8]])
            Sx.op("act", lambda e, dst=dst: e.activation(out=self.fp(dst)[:, 0:512], in_=r, func=AF.Sin),
                  reads=[self.fpb[8]], writes=[self.fpb[dst]])

    def retention(self, seq, blk):
        Sx = self.S_
        b = seq
        self.norm(0, b)
        self.rope_tables(seq, blk)
        cos = self.fp(3)[:, 0:512]
        sin = self.fp(4)[:, 0:512]
        BQT, BQD, BKT, BKD, BV, BSG, BST, BON, BONT, BSB = 2, 4, 6, 8, 10, 14, 18, 22, 26, 30

        def wout():
            wv, wb = self.w_get()
            for ec in range(8):
                ps = self.ps_alloc()
                for vc in range(4):
                    Sx.op("pe", lambda e, ec=ec, vc=vc, ps=ps: e.matmul(
                        ps.ap, wv[:, vc, ec * 128:(ec + 1) * 128], self.bp(BONT + vc), start=(vc == 0), stop=(vc == 3)),
                        reads=[wb, self.bpb[BONT + vc]], writes=[ps.buf], signal=(vc == 3))
                Sx.op("dve", lambda e, ec=ec, ps=ps: e.scalar_tensor_tensor(
                    out=self.hc(ec), in0=ps.ap, scalar=self.mcol(16 + ec, b), in1=self.hc(ec),
                    op0=ALU.mult, op1=ALU.add), reads=[ps.buf, self.hb[ec], self.cbuf], writes=[self.hb[ec]])
                self.ps_free(ps)
            self.w_done()

        for hd in range(RET_H):
            wv, wb = self.w_get()
            pq = [self.ps_alloc() for _ in range(4)]
            if hd == 0:
                self.proj_fm_multi(wv, wb, [cc * 128 for cc in range(4)], pq)
            else:
                for cc in range(4):
                    self.proj_fm(wv, wb, cc * 128, pq[cc])
            self.w_done()
            wv, wb = self.w_get()
            for i in range(4):
                ps = self.ps_alloc()
                self.proj_tm(wv, wb, i, ps)
                Sx.op("act", lambda e, i=i, ps=ps: e.activation(out=self.bp(BV + i), in_=ps.ap, func=AF.Copy),
                      reads=[ps.buf], writes=[self.bpb[BV + i]])
                self.ps_free(ps)
            self.w_done()
            wv, wb = self.w_get()
            for i in range(4):
                ps = self.ps_alloc()
                self.proj_tm(wv, wb, i, ps)
                Sx.op("act", lambda e, i=i, ps=ps: e.activation(out=self.bp(BSG + i), in_=ps.ap, func=AF.Silu),
                      reads=[ps.buf], writes=[self.bpb[BSG + i]])
                self.ps_free(ps)
            self.w_done()
            for (p_lo, p_hi, dst) in ((pq[0], pq[1], BQT), (pq[2], pq[3], BKT)):
                Sx.op("dve", lambda e, p=p_lo: e.tensor_tensor(self.fp(5)[:, 0:512], p.ap, cos, op=ALU.mult),
                      reads=[p_lo.buf, self.fpb[3]], writes=[self.fpb[5]])
                Sx.op("dve", lambda e, p=p_hi: e.tensor_tensor(self.fp(6)[:, 0:512], p.ap, sin, op=ALU.mult),
                      reads=[p_hi.buf, self.fpb[4]], writes=[self.fpb[6]])
                Sx.op("pool", lambda e, dst=dst: e.tensor_tensor(self.bp(dst), self.fp(5)[:, 0:512], self.fp(6)[:, 0:512],
                                                                  op=ALU.subtract),
                      reads=[self.fpb[5], self.fpb[6]], writes=[self.bpb[dst]])
                Sx.op("dve", lambda e, p=p_lo: e.tensor_tensor(self.fp(7)[:, 0:512], p.ap, sin, op=ALU.mult),
                      reads=[p_lo.buf, self.fpb[4]], writes=[self.fpb[7]])
                Sx.op("dve", lambda e, p=p_hi: e.tensor_tensor(self.fp(8)[:, 0:512], p.ap, cos, op=ALU.mult),
                      reads=[p_hi.buf, self.fpb[3]], writes=[self.fpb[8]])
                Sx.op("pool", lambda e, dst=dst: e.tensor_tensor(self.bp(dst + 1), self.fp(7)[:, 0:512], self.fp(8)[:, 0:512],
                                                                  op=ALU.add),
                      reads=[self.fpb[7], self.fpb[8]], writes=[self.bpb[dst + 1]])
            for p in pq:
                self.ps_free(p)
            for dc in range(2):
                Sx.op("pool", lambda e, dc=dc: e.tensor_tensor(self.bp(BQD + dc), self.bp(BQT + dc),
                                                                 self.qdec[:, hd * 512:(hd + 1) * 512], op=ALU.mult),
                      reads=[self.bpb[BQT + dc], self.cbuf], writes=[self.bpb[BQD + dc]])
            pt = self.ps_alloc()
            for i in range(4):
                for dc in range(2):
                    Sx.op("pe", lambda e, i=i, dc=dc: e.transpose(
                        pt.bf[:, (i * 2 + dc) * 128:(i * 2 + dc + 1) * 128],
                        self.bp(BKT + dc)[:, i * 128:(i + 1) * 128], self.cb(CB_ID)),
                        reads=[self.bpb[BKT + dc], self.cbuf], writes=[pt.buf])
            kd = self.bp(BKD, 2)
            for i in range(4):
                Sx.op("act", lambda e, i=i: e.activation(out=kd[:, i * 256:(i + 1) * 256], in_=pt.bf[:, i * 256:(i + 1) * 256],
                                                         func=AF.Identity, scale=self.kdec[:, hd * 4 + i:hd * 4 + i + 1]),
                      reads=[pt.buf, self.cbuf], writes=[self.bpb[BKD], self.bpb[BKD + 1]])
            self.ps_free(pt)
            for j in range(4):
                n = 512 - 128 * j
                ps = self.ps_alloc()
                for dc in range(2):
                    Sx.op("pe", lambda e, j=j, dc=dc, n=n, ps=ps: e.matmul(
                        ps.ap[:, 0:n], self.bp(BKT + dc)[:, j * 128:(j + 1) * 128], self.bp(BQT + dc)[:, j * 128:512],
                        start=(dc == 0), stop=(dc == 1)),
                        reads=[self.bpb[BKT + dc], self.bpb[BQT + dc]], writes=[ps.buf], signal=(dc == 1))
                Sx.op("dve", lambda e, j=j, n=n, ps=ps: e.tensor_tensor(
                    self.bp(BST + j)[:, 0:n], ps.ap[:, 0:n], self.retmask[:, hd * 512:hd * 512 + n], op=ALU.mult),
                    reads=[ps.buf, self.cbuf], writes=[self.bpb[BST + j]])
                self.ps_free(ps)
            for dc in range(2):
                Sx.op("act", lambda e, dc=dc: e.activation(out=self.bp(BSB + dc), in_=self.state[:, (hd * 2 + dc) * 512:(hd * 2 + dc + 1) * 512], func=AF.Copy),
                      reads=[self.stb[hd * 2 + dc]], writes=[self.bpb[BSB + dc]])
            po = [self.ps_alloc() for _ in range(4)]
            Sx.op("dve", lambda e: e.memset(self.gn[:], 0.0), writes=[self.gnb])
            for c in range(4):
                nmm = c + 1 + 2
                idx = 0
                for j in range(c + 1):
                    Sx.op("pe", lambda e, c=c, j=j, idx=idx, nmm=nmm: e.matmul(
                        po[c].ap, self.bp(BST + j)[:, (c - j) * 128:(c - j + 1) * 128], self.bp(BV + j),
                        start=(idx == 0), stop=False),
                        reads=[self.bpb[BST + j], self.bpb[BV + j]], writes=[po[c].buf], signal=False)
                    idx += 1
                for dc in range(2):
                    Sx.op("pe", lambda e, c=c, dc=dc: e.matmul(
                        po[c].ap, self.bp(BQD + dc)[:, c * 128:(c + 1) * 128], self.bp(BSB + dc),
                        start=False, stop=(dc == 1)),
                        reads=[self.bpb[BQD + dc], self.bpb[BSB + dc]], writes=[po[c].buf], signal=(dc == 1))
                Sx.op("act", lambda e, c=c: e.activation(out=self.bp(1), in_=po[c].ap, func=AF.Square,
                                                         accum_out=self.gn[:, c:c + 1]),
                      reads=[po[c].buf], writes=[self.bpb[1], self.gnb])
            Sx.op("act", lambda e: e.activation(out=self.gn[:, 4:8], in_=self.gn[:, 0:4], func=AF.Ln, scale=1.0 / 512.0, bias=EPS),
                  reads=[self.gnb], writes=[self.gnb])
            Sx.op("act", lambda e: e.activation(out=self.gn[:, 4:8], in_=self.gn[:, 4:8], func=AF.Exp, scale=-0.5),
                  reads=[self.gnb], writes=[self.gnb])
            for c in range(4):
                Sx.op("dve", lambda e, c=c: e.scalar_tensor_tensor(
                    out=self.bp(BON + c), in0=po[c].ap, scalar=self.gn[:, 4 + c:5 + c], in1=self.bp(BSG + c),
                    op0=ALU.mult, op1=ALU.mult),
                    reads=[po[c].buf, self.gnb, self.bpb[BSG + c]], writes=[self.bpb[BON + c]])
                self.ps_free(po[c])
            if hd > 0:
                wout()
            gC = GAMMAS[hd] ** 512
            for dc in range(2):
                ps = self.ps_alloc()
                for j in range(4):
                    Sx.op("pe", lambda e, j=j, dc=dc, ps=ps: e.matmul(
                        ps.ap, kd[:, j * 256 + dc * 128:j * 256 + (dc + 1) * 128], self.bp(BV + j),
                        start=(j == 0), stop=(j == 3)),
                        reads=[self.bpb[BKD], self.bpb[BKD + 1], self.bpb[BV + j]], writes=[ps.buf], signal=(j == 3))
                si = hd * 2 + dc
                st = self.state[:, si * 512:(si + 1) * 512]
                Sx.op("dve", lambda e, st=st, ps=ps: e.scalar_tensor_tensor(out=st, in0=st, scalar=gC, in1=ps.ap,
                                                                           op0=ALU.mult, op1=ALU.add),
                      reads=[ps.buf, self.stb[si]], writes=[self.stb[si]])
                self.ps_free(ps)
            for vc in range(4):
                pt = self.ps_alloc()
                for c in range(4):
                    Sx.op("pe", lambda e, vc=vc, c=c, pt=pt: e.transpose(
                        pt.bf[:, c * 128:(c + 1) * 128], self.bp(BON + c)[:, vc * 128:(vc + 1) * 128], self.cb(CB_ID)),
                        reads=[self.bpb[BON + c], self.cbuf], writes=[pt.buf], signal=(c == 3))
                if vc % 2 == 0:
                    Sx.op("act", lambda e, vc=vc, pt=pt: e.activation(out=self.bp(BONT + vc), in_=pt.bf[:, 0:512], func=AF.Copy),
                          reads=[pt.buf], writes=[self.bpb[BONT + vc]])
                else:
                    Sx.op("dve", lambda e, vc=vc, pt=pt: e.tensor_copy(self.bp(BONT + vc), pt.bf[:, 0:512]),
                          reads=[pt.buf], writes=[self.bpb[BONT + vc]])
                self.ps_free(pt)

        wout()

    def ffn(self, l, seq, blk):
        Sx = self.S_
        b = seq
        self.norm(1 if l == 0 else 4, b)
        BACT = 2
        pend = None

        def finish(p):
            fc, yv, yvi, yg, ygi = p
            Sx.op("act", lambda e: e.activation(out=yg, in_=yg, func=AF.Silu),
                  reads=[self.fpb[ygi]], writes=[self.fpb[ygi]])
            Sx.op("dve", lambda e: e.tensor_tensor(self.bp(BACT + fc), yv, yg, op=ALU.mult),
                  reads=[self.fpb[yvi], self.fpb[ygi]], writes=[self.bpb[BACT + fc]])

        for j in range(11):
            wv, wb = self.w_get()
            pre = None
            if j == 0:
                pre = {(s_, g_): self.ps_alloc() for s_ in range(2) for g_ in range(2)}
                keys = list(pre.keys())
                self.proj_fm_multi(wv, wb, [g_ * 256 + s_ * 128 for (s_, g_) in keys], [pre[k_] for k_ in keys])
            for s in range(2):
                fc = 2 * j + s
                ys = []
                for gi in range(2):
                    ui_ = 3 + gi * 2 + (fc % 2)
                    hcol_ = (l * 44 + gi * 22 + fc) * 2
                    Sx.op("pool", lambda e, ui_=ui_, hcol_=hcol_: e.tensor_copy(self.fp(ui_)[:, 0:2], self.halo[:, hcol_:hcol_ + 2]),
                          reads=[self.halob], writes=[self.ubh[ui_]])
                for gi in range(2):
                    ci = gi * 22 + fc
                    if pre is not None:
                        ps = pre[(s, gi)]
                    else:
                        ps = self.ps_alloc()
                        self.proj_fm(wv, wb, gi * 256 + s * 128, ps)
                    ui = 3 + gi * 2 + (fc % 2)
                    yi = 7 + gi * 2 + (fc % 2)
                    ub = self.fp(ui)
                    y = self.fp(yi)[:, 0:512]
                    hcol = (l * 44 + ci) * 2
                    w0 = self.vcol(VC_CONVW + (l * 3 + 0) * 44 + ci)
                    w1 = self.vcol(VC_CONVW + (l * 3 + 1) * 44 + ci)
                    w2 = self.vcol(VC_CONVW + (l * 3 + 2) * 44 + ci)
                    bb = self.vcol(VC_CONVB + l * 44 + ci)
                    Sx.op("act", lambda e, ub=ub, ps=ps: e.activation(out=ub[:, 2:514], in_=ps.ap, func=AF.Copy),
                          reads=[ps.buf], writes=[self.fpb[ui]])
                    Sx.op("act", lambda e, y=y, ps=ps, w2=w2, bb=bb: e.activation(out=y, in_=ps.ap, func=AF.Identity,
                                                                                 scale=w2, bias=bb),
                          reads=[ps.buf, self.cbuf], writes=[self.fpb[yi]])
                    self.ps_free(ps)
                    Sx.op("dve", lambda e, y=y, ub=ub, w1=w1: e.scalar_tensor_tensor(
                        out=y, in0=ub[:, 1:513], scalar=w1, in1=y, op0=ALU.mult, op1=ALU.add),
                        reads=[self.fpb[ui], self.ubh[ui], self.fpb[yi], self.cbuf], writes=[self.fpb[yi]])
                    Sx.op("dve", lambda e, y=y, ub=ub, w0=w0: e.scalar_tensor_tensor(
                        out=y, in0=ub[:, 0:512], scalar=w0, in1=y, op0=ALU.mult, op1=ALU.add),
                        reads=[self.fpb[ui], self.ubh[ui], self.fpb[yi], self.cbuf], writes=[self.fpb[yi]])
                    Sx.op("pool", lambda e, ub=ub, hcol=hcol: e.tensor_copy(self.halo[:, hcol:hcol + 2], ub[:, 512:514]),
                          reads=[self.fpb[ui]], writes=[self.halob])
                    ys.append((y, yi))
                    if gi == 0 and pend is not None:
                        finish(pend)
                        pend = None
                pend = (fc, ys[0][0], ys[0][1], ys[1][0], ys[1][1])
            self.w_done()
        finish(pend)
        G = 3
        gt = []
        for g in range(G):
            idx = self.wnext + g
            assert idx < self.wissued
            slot = idx % NSLOT
            dsc = self.wtiles[idx % self.NWT]
            nn = dsc["nrc"] * dsc["ncols"]
            gt.append((self.wring[:, slot * 4096:slot * 4096 + nn].rearrange("p (k e) -> p k e", e=dsc["ncols"]),
                       self.wslot_buf[slot], self.ps_alloc()))
        for fc in range(NFC):
            for (gv, gb, gps) in gt:
                Sx.op("pe", lambda e, fc=fc, gv=gv, gps=gps: e.matmul(gps.ap, gv[:, fc, :], self.bp(BACT + fc),
                                                                      start=(fc == 0), stop=(fc == NFC - 1)),
                      reads=[gb, self.bpb[BACT + fc]], writes=[gps.buf], signal=(fc == NFC - 1))
        for ec, (gv, gb, gps) in enumerate(gt):
            self.w_done()
            Sx.op("dve", lambda e, ec=ec, gps=gps: e.scalar_tensor_tensor(
                out=self.hc(ec), in0=gps.ap, scalar=self.mcol(l * 48 + 40 + ec, b), in1=self.hc(ec),
                op0=ALU.mult, op1=ALU.add), reads=[gps.buf, self.hb[ec], self.cbuf], writes=[self.hb[ec]])
            self.ps_free(gps)
        for ec in range(G, 8):
            wv, wb = self.w_get()
            ps = self.ps_alloc()
            for fc in range(NFC):
                Sx.op("pe", lambda e, fc=fc, ps=ps: e.matmul(ps.ap, wv[:, fc, :], self.bp(BACT + fc),
                                                             start=(fc == 0), stop=(fc == NFC - 1)),
                      reads=[wb, self.bpb[BACT + fc]], writes=[ps.buf], signal=(fc == NFC - 1))
            self.w_done()
            Sx.op("dve", lambda e, ec=ec, ps=ps: e.scalar_tensor_tensor(
                out=self.hc(ec), in0=ps.ap, scalar=self.mcol(l * 48 + 40 + ec, b), in1=self.hc(ec),
                op0=ALU.mult, op1=ALU.add), reads=[ps.buf, self.hb[ec], self.cbuf], writes=[self.hb[ec]])
            self.ps_free(ps)

    def headnorm_a(self, ps, par):
        Sx = self.S_
        Sx.op("act", lambda e: e.activation(out=self.bp(par), in_=ps.ap, func=AF.Square),
              reads=[ps.buf], writes=[self.bpb[par]])

    def headnorm_b(self, ps, par, gain_col, dst_ap, dst_bufs, lnbias):
        Sx = self.S_
        p2 = self.ps_alloc()
        Sx.op("pe", lambda e: e.matmul(p2.ap, self.cb(CB_BD), self.bp(par), start=True, stop=True),
              reads=[self.bpb[par], self.cbuf], writes=[p2.buf])
        rs = self.fp(1 + par)[:, 0:512]
        Sx.op("act", lambda e: e.activation(out=rs, in_=p2.ap, func=AF.Ln, scale=1.0 / 64.0, bias=EPS),
              reads=[p2.buf], writes=[self.fpb[1 + par]])
        self.ps_free(p2)
        Sx.op("act", lambda e: e.activation(out=rs, in_=rs, func=AF.Exp, scale=-0.5, bias=lnbias),
              reads=[self.fpb[1 + par]], writes=[self.fpb[1 + par]])
        Sx.op("dve", lambda e: e.scalar_tensor_tensor(out=dst_ap, in0=ps.ap, scalar=self.vcol(gain_col), in1=rs,
                                                      op0=ALU.mult, op1=ALU.mult),
              reads=[ps.buf, self.fpb[1 + par], self.cbuf], writes=dst_bufs)
        self.ps_free(ps)

    def headnorm_proj(self, gain_col, lnbias, dst_fn):
        pend = None
        for tix in range(2):
            wv, wb = self.w_get()
            pre = None
            if tix == 0:
                pre = [self.ps_alloc() for _ in range(4)]
                self.proj_fm_multi(wv, wb, [cc * 128 for cc in range(4)], pre)
            for cc in range(4):
                qc = tix * 4 + cc
                if pre is not None:
                    ps = pre[cc]
                else:
                    ps = self.ps_alloc()
                    self.proj_fm(wv, wb, cc * 128, ps)
                self.headnorm_a(ps, qc % 2)
                if pend is not None:
                    self.headnorm_b(*pend)
                dst_ap, dst_bufs = dst_fn(qc)
                pend = (ps, qc % 2, gain_col, dst_ap, dst_bufs, lnbias)
            self.w_done()
        self.headnorm_b(*pend)

    def kv_proj(self, seq, blk):
        Sx = self.S_
        b = seq
        t0 = blk * T
        self.norm(2, b)
        self.headnorm_proj(VC_KG, 0.0,
                           lambda kc: (self.KT[:, kc * self.S + t0:kc * self.S + t0 + T], [self.KTb[kc]]))
        for vt in range(2):
            wv, wb = self.w_get()
            for i in range(4):
                ps = self.ps_alloc()
                self.proj_tm(wv, wb, i, ps)
                ti = blk * 4 + i
                if i % 2 == 0:
                    Sx.op("act", lambda e, ti=ti, vt=vt, ps=ps: e.activation(
                        out=self.V[:, ti * D + vt * 512:ti * D + (vt + 1) * 512], in_=ps.ap, func=AF.Copy),
                        reads=[ps.buf], writes=[self.Vb[ti]])
                else:
                    Sx.op("dve", lambda e, ti=ti, vt=vt, ps=ps: e.tensor_copy(
                        self.V[:, ti * D + vt * 512:ti * D + (vt + 1) * 512], ps.ap),
                        reads=[ps.buf], writes=[self.Vb[ti]])
                self.ps_free(ps)
            self.w_done()

    def attention(self, seq, blk):
        Sx = self.S_
        b = seq
        q0 = blk * T
        self.norm(3, b, reuse_rstd=True)
        BQ, BO, BSP, BA, BNC = 2, 10, 18, 24, 28
        FE = 3
        self.headnorm_proj(VC_QG, math.log(0.125), lambda qc: (self.bp(BQ + qc), [self.bpb[BQ + qc]]))
        for i in range(2):
            Sx.op("pool", lambda e, i=i: e.memset(self.bp(BNC + i), 0.0), writes=[self.bpb[BNC + i]])
        nkb = (q0 + T) // 128
        groups = []
        a = nkb - 1
        while a >= 0:
            groups.append([a, a - 1] if a - 1 >= 0 else [a])
            a -= 2
        ng = len(groups)
        assert sorted(self.free_banks) == list(range(8)), self.free_banks
        items = [(hd, gi) for hd in range(SB_H) for gi in range(ng)]
        n_it = len(items)
        O = self.ps_take(6)
        CS = self.ps_take(7)
        info = {}

        def zero(bank):
            Sx.op("pe", lambda e: e.matmul(bank.ap, self.cb(CB_ZERO), self.hnc(0), start=True, stop=False),
                  reads=[self.cbuf, self.hnb[0]], writes=[bank.buf], signal=False)

        def s1(i):
            hd, gi = items[i]
            c = hd // 2
            po = (hd % 2) * 64
            grp = groups[gi]
            zp = (i % 3) * 2
            ep = FE + (i % 2) * 2
            sp0 = BSP + (i % 3) * 2
            blks = []
            for s, a in enumerate(grp):
                c0 = max(0, 128 * a - q0)
                Z = self.ps_take(zp + s)
                blks.append((a, Z, c0, s))
                diag = (128 * a >= q0)
                Sx.op("pe", lambda e, a=a, Z=Z, c0=c0: e.matmul(
                    Z.ap[:, c0:512], self.KT[po:po + 64, c * self.S + a * 128:c * self.S + (a + 1) * 128],
                    self.bp(BQ + c)[po:po + 64, c0:512], start=True, stop=False),
                    reads=[self.KTb[c], self.bpb[BQ + c]], writes=[Z.buf], signal=not diag)
                if diag:
                    Sx.op("pe", lambda e, Z=Z, c0=c0: e.matmul(Z.ap[:, c0:c0 + 128], self.cb(CB_ID), self.cb(CB_DMASK),
                                                               start=False, stop=False),
                          reads=[self.cbuf], writes=[Z.buf])
            merged = (len(grp) == 2)
            c0m = min(bk[2] for bk in blks)
            info[i] = (blks, merged, zp, ep, sp0)
            if merged:
                zz = self.pst[:, zp * 512:(zp + 2) * 512].rearrange("p (a b) -> p a b", b=512)[:, :, c0m:512]
                ee = self.fpool[:, ep * 520:ep * 520 + 1024].rearrange("p (a b) -> p a b", b=512)[:, :, c0m:512]
                ss = self.bp(sp0, 2).rearrange("p (a b) -> p a b", b=512)[:, :, c0m:512]
                Sx.op("act", lambda e: e.activation(out=ee, in_=zz, func=AF.Exp),
                      reads=[blks[0][1].buf, blks[1][1].buf], writes=[self.fpb[ep], self.fpb[ep + 1]])
                Sx.op("act", lambda e: e.activation(out=ss, in_=ee, func=AF.Ln, bias=1.0),
                      reads=[self.fpb[ep], self.fpb[ep + 1]], writes=[self.bpb[sp0], self.bpb[sp0 + 1]])
            else:
                for (a, Z, c0, s) in blks:
                    Sx.op("act", lambda e, Z=Z, c0=c0, s=s: e.activation(
                        out=self.fp(ep + s)[:, c0:512], in_=Z.ap[:, c0:512], func=AF.Exp),
                        reads=[Z.buf], writes=[self.fpb[ep + s]])
                    Sx.op("act", lambda e, c0=c0, s=s: e.activation(
                        out=self.bp(sp0 + s)[:, c0:512], in_=self.fp(ep + s)[:, c0:512], func=AF.Ln, bias=1.0),
                        reads=[self.fpb[ep + s]], writes=[self.bpb[sp0 + s]])

        def s2(i):
            hd, gi = items[i]
            c = hd // 2
            blks, merged, zp, ep, sp0 = info[i]
            a0 = BA + (i % 2) * 2
            if gi == 0:
                zero(CS)
            for bi, (a, Z, c0, s) in enumerate(blks):
                has_ones = any(a2 > a for (a2, _z, _c, _s) in blks)
                has_carry = gi > 0
                Sx.op("pe", lambda e, Z=Z, c0=c0, s=s, st=not (has_ones or has_carry): e.matmul(
                    Z.ap[:, c0:512], self.cb(CB_NEGU), self.bp(sp0 + s)[:, c0:512], start=False, stop=st),
                    reads=[self.cbuf, self.bpb[sp0 + s]], writes=[Z.buf], signal=False)
                for (a2, Z2, c02, s2_) in blks:
                    if a2 > a:
                        Sx.op("pe", lambda e, Z=Z, c02=c02, s2_=s2_, st=not has_carry: e.matmul(
                            Z.ap[:, c02:512], self.cb(CB_NEGONES), self.bp(sp0 + s2_)[:, c02:512],
                            start=False, stop=st),
                            reads=[self.cbuf, self.bpb[sp0 + s2_]], writes=[Z.buf], signal=False)
                if has_carry:
                    nci = BNC + (gi - 1) % 2
                    Sx.op("pe", lambda e, Z=Z, nci=nci, c0=c0: e.matmul(
                        Z.ap[:, c0:512], self.cb(CB_SEL), self.bp(nci)[:, c0:512], start=False, stop=True),
                        reads=[self.cbuf, self.bpb[nci]], writes=[Z.buf], signal=False)
                cs_last = (gi == ng - 1) and (bi == len(blks) - 1)
                Sx.op("pe", lambda e, c0=c0, s=s, cs_last=cs_last: e.matmul(
                    CS.ap[:, c0:512], self.cb(CB_ONES), self.bp(sp0 + s)[:, c0:512], start=False, stop=cs_last),
                    reads=[self.cbuf, self.bpb[sp0 + s]], writes=[CS.buf, Z.buf])
            for _ in range(NFILL):
                Sx.op("pe", lambda e: e.matmul(CS.ap, self.cb(CB_ZERO), self.bp(BQ + c), start=False, stop=False),
                      reads=[self.cbuf, self.bpb[BQ + c]], writes=[CS.buf], signal=False)
            if merged:
                c0m = min(bk[2] for bk in blks)
                zz = self.pst[:, zp * 512:(zp + 2) * 512].rearrange("p (a b) -> p a b", b=512)[:, :, c0m:512]
                aa = self.bp(a0, 2).rearrange("p (a b) -> p a b", b=512)[:, :, c0m:512]
                Sx.op("act", lambda e: e.activation(out=aa, in_=zz, func=AF.Exp),
                      reads=[blks[0][1].buf, blks[1][1].buf], writes=[self.bpb[a0], self.bpb[a0 + 1]])
            else:
                for (a, Z, c0, s) in blks:
                    Sx.op("act", lambda e, Z=Z, c0=c0, s=s: e.activation(
                        out=self.bp(a0 + s)[:, c0:512], in_=Z.ap[:, c0:512], func=AF.Exp),
                        reads=[Z.buf], writes=[self.bpb[a0 + s]])
            for (a, Z, c0, s) in blks:
                self.ps_free(Z)
            if gi + 1 < ng:
                nci = BNC + gi % 2
                Sx.op("dve", lambda e, nci=nci: e.tensor_scalar_mul(self.bp(nci)[0:33, :], CS.ap[0:33, :], -1.0),
                      reads=[CS.buf], writes=[self.bpb[nci]])
                Sx.op("dve", lambda e, nci=nci: e.scalar_tensor_tensor(
                    out=self.bp(nci)[32:33, :], in0=CS.ap[32:33, :], scalar=-1.0, in1=self.bp(nci)[32:33, :],
                    op0=ALU.mult, op1=ALU.subtract),
                    reads=[CS.buf, self.bpb[nci]], writes=[self.bpb[nci]])

        def s3(i):
            hd, gi = items[i]
            c = hd // 2
            po = (hd % 2) * 64
            blks, merged, zp, ep, sp0 = info[i]
            a0 = BA + (i % 2) * 2
            if gi == 0:
                zero(O)
            for bi, (a, Z, c0, s) in enumerate(blks):
                o_last = (gi == ng - 1) and (bi == len(blks) - 1)
                Sx.op("pe", lambda e, a=a, c0=c0, s=s, o_last=o_last: e.matmul(
                    O.ap[:, c0:512], self.V[:, a * D + c * 128:a * D + (c + 1) * 128], self.bp(a0 + s)[:, c0:512],
                    start=False, stop=o_last),
                    reads=[self.Vb[a], self.bpb[a0 + s]], writes=[O.buf])
            if gi == ng - 1:
                Sx.op("dve", lambda e: e.tensor_copy(self.bp(BO + c)[po:po + 64, :], O.ap[po:po + 64, :]),
                      reads=[O.buf], writes=[self.bpb[BO + c]])

        s1(0)
        if n_it > 1:
            s1(1)
        s2(0)
        for i in range(n_it):
            if i + 2 < n_it:
                s1(i + 2)
            if i + 1 < n_it:
                s2(i + 1)
            s3(i)
        self.ps_free(O)
        self.ps_free(CS)
        for tix in range(2):
            wv, wb = self.w_get()
            for cc in range(4):
                ec = tix * 4 + cc
                ps = self.ps_alloc()
                for c in range(8):
                    Sx.op("pe", lambda e, c=c, cc=cc, ps=ps: e.matmul(ps.ap, wv[:, c, cc * 128:(cc + 1) * 128], self.bp(BO + c),
                                                                     start=(c == 0), stop=(c == 7)),
                          reads=[wb, self.bpb[BO + c]], writes=[ps.buf], signal=(c == 7))
                Sx.op("dve", lambda e, ec=ec, ps=ps: e.scalar_tensor_tensor(
                    out=self.hc(ec), in0=ps.ap, scalar=self.mcol(48 + 16 + ec, b), in1=self.hc(ec),
                    op0=ALU.mult, op1=ALU.add), reads=[ps.buf, self.hb[ec], self.cbuf], writes=[self.hb[ec]])
                self.ps_free(ps)
            self.w_done()

    def _skip_w(self, n):
        for _ in range(n):
            self.w_get()
            self.w_done()

    def _block(self, seq, blk):
        st = self.stage
        self.load_x(seq, blk)
        self.retention(seq, blk)
        if st == "l0mix":
            self._skip_w(self.NWT - 16)
            self.store_out(seq, blk)
            return
        self.ffn(0, seq, blk)
        if st == "l0":
            self._skip_w(self.NWT - 35)
            self.store_out(seq, blk)
            return
        self.kv_proj(seq, blk)
        self.attention(seq, blk)
        if st == "l1mix":
            self._skip_w(19)
            self.store_out(seq, blk)
            return
        self.ffn(1, seq, blk)
        self.store_out(seq, blk)

    def _finish(self):
        Sx = self.S_
        for fi in range(3, 7):
            Sx.wait_tok("sp", (f"f{fi}", Sx.cnt[f"f{fi}"]))


_CONSTS = None


def kernel(**inputs):
    global _CONSTS
    if _CONSTS is None:
        _CONSTS = host_consts()
    x = np.ascontiguousarray(np.asarray(inputs["x"], dtype=np.float32))
    B, S, _ = x.shape
    nseq = B // NCORES
    bld = Builder(nseq=nseq, S=S, stage="full")
    nc = bld.build()
    in_maps = []
    for core in range(NCORES):
        sl = slice(core * nseq, (core + 1) * nseq)
        m = {"x": np.ascontiguousarray(x[sl]),
             "c": np.ascontiguousarray(np.asarray(inputs["c"], np.float32)[sl]),
             "positions": np.ascontiguousarray(np.asarray(inputs["positions"]).astype(np.int32)[sl])}
        for k in W_SHAPES:
            m[k] = np.ascontiguousarray(np.asarray(inputs[k], dtype=np.float32))
        m.update(_CONSTS)
        in_maps.append(m)
    res = run_bass_kernel_spmd(nc, in_maps, core_ids=list(range(NCORES)))
    out = np.concatenate([np.asarray(r["out"]) for r in res.results], axis=0)
    return out.astype(np.float32)
```
